# Optimizing a Trainium2 kernel written in Bass

```python
import math
import jax, jax.numpy as jnp
from jax import lax
import numpy as np

D_MODEL = 2048
BATCH = 8
SEQ = 4096
DEPTH = 1

N_META = 16
BLOCK_Q = 128
SB_HEADS = 8
SB_HEAD_DIM = 128
SB_WIDTH = SB_HEADS * SB_HEAD_DIM
DIFF_HEADS = 8
DIFF_QK_DIM = 64
DIFF_V_DIM = 2 * DIFF_QK_DIM
DIFF_QK_WIDTH = DIFF_HEADS * 2 * DIFF_QK_DIM
DIFF_WIDTH = DIFF_HEADS * DIFF_V_DIM
MIX_WIDTH = SB_WIDTH + DIFF_WIDTH
IN_SIZES = (SB_WIDTH, SB_WIDTH, SB_WIDTH, SB_WIDTH,
            DIFF_QK_WIDTH, DIFF_QK_WIDTH, DIFF_WIDTH, DIFF_WIDTH)
IN_WIDTH = 4 * SB_WIDTH + 2 * DIFF_QK_WIDTH + 2 * DIFF_WIDTH
ROPE_THETA = 500000.0
ROT_DIM = DIFF_QK_DIM // 4
RMS_EPS = 1e-6
SUBLN_EPS = 1e-5

kernel_name = "hybrid_stickbreak_diffattn_layer"


def rms_norm(x, w, eps=RMS_EPS):
    xf = x.astype(jnp.float32)
    y = xf * lax.rsqrt(jnp.mean(xf * xf, axis=-1, keepdims=True) + eps)
    return (y * w.astype(jnp.float32)).astype(x.dtype)


def lambda_init_for(layer):
    return 0.8 - 0.6 * math.exp(-0.3 * layer)


def partial_rope(x, cos, sin):
    half = ROT_DIM // 2
    c = cos[None, :, None, None, :]
    s = sin[None, :, None, None, :]
    x1 = x[..., :half]
    x2 = x[..., half:ROT_DIM]
    return jnp.concatenate([x1 * c - x2 * s, x2 * c + x1 * s, x[..., ROT_DIM:]], axis=-1)


def stick_breaking_block(q, k, v, q_pos, k_pos):
    z = jnp.einsum('bhqd,bhkd->bhqk', q, k) / math.sqrt(q.shape[-1])
    mask = k_pos[None, :] < q_pos[:, None]
    log_beta = jax.nn.log_sigmoid(z)
    log_keep = jnp.where(mask, jax.nn.log_sigmoid(-z), 0.0)
    tail = lax.cumsum(log_keep, axis=3, reverse=True) - log_keep
    a = jnp.where(mask, jnp.exp(log_beta + tail), 0.0)
    return jnp.einsum('bhqk,bhkd->bhqd', a, v)


def diff_attention_block(q, k, v, lam, q_pos, k_pos):
    s = jnp.einsum('bhcqd,bhckd->bhcqk', q, k) / math.sqrt(q.shape[-1])
    mask = k_pos[None, :] <= q_pos[:, None]
    p = jax.nn.softmax(jnp.where(mask, s, -jnp.inf), axis=-1)
    w = p[:, :, 0] - lam * p[:, :, 1]
    return jnp.einsum('bhqk,bhkd->bhqd', w, v)


def setup_inputs(seed: int = 0) -> dict:
    key = jax.random.key(seed)
    ks = jax.random.split(key, 13)
    f32 = jnp.float32
    x = jax.random.normal(ks[0], (BATCH, SEQ, D_MODEL), f32)
    meta = jax.random.normal(ks[1], (N_META, D_MODEL), f32)
    norm_w = 1.0 + 0.02 * jax.random.normal(ks[2], (DEPTH, D_MODEL), f32)
    w_in = jax.random.normal(ks[3], (DEPTH, D_MODEL, IN_WIDTH), f32) * D_MODEL ** -0.5
    q_norm_w = 1.0 + 0.02 * jax.random.normal(ks[4], (DEPTH, DIFF_QK_DIM), f32)
    k_norm_w = 1.0 + 0.02 * jax.random.normal(ks[5], (DEPTH, DIFF_QK_DIM), f32)
    lambda_q1 = 0.1 * jax.random.normal(ks[6], (DEPTH, DIFF_QK_DIM), f32)
    lambda_k1 = 0.1 * jax.random.normal(ks[7], (DEPTH, DIFF_QK_DIM), f32)
    lambda_q2 = 0.1 * jax.random.normal(ks[8], (DEPTH, DIFF_QK_DIM), f32)
    lambda_k2 = 0.1 * jax.random.normal(ks[9], (DEPTH, DIFF_QK_DIM), f32)
    subln_w = 1.0 + 0.02 * jax.random.normal(ks[10], (DEPTH, DIFF_V_DIM), f32)
    w_out = jax.random.normal(ks[11], (DEPTH, MIX_WIDTH, D_MODEL), f32) * MIX_WIDTH ** -0.5
    return {"x": x, "meta": meta, "norm_w": norm_w, "w_in": w_in,
            "q_norm_w": q_norm_w, "k_norm_w": k_norm_w,
            "lambda_q1": lambda_q1, "lambda_k1": lambda_k1,
            "lambda_q2": lambda_q2, "lambda_k2": lambda_k2,
            "subln_w": subln_w, "w_out": w_out}


def reference(x, meta, norm_w, w_in, q_norm_w, k_norm_w, lambda_q1, lambda_k1,
              lambda_q2, lambda_k2, subln_w, w_out):
    f32 = jnp.float32
    b, seq = x.shape[0], x.shape[1]
    h = jnp.concatenate(
        [jnp.broadcast_to(meta[None].astype(x.dtype), (b, N_META, D_MODEL)), x], axis=1)
    t_all = h.shape[1]
    pos = jnp.arange(t_all, dtype=f32)
    inv_freq = ROPE_THETA ** (-jnp.arange(0, ROT_DIM, 2, dtype=f32) / ROT_DIM)
    ang = pos[:, None] * inv_freq[None, :]
    cos, sin = jnp.cos(ang), jnp.sin(ang)
    split_at = list(np.cumsum(IN_SIZES)[:-1])
    real_blocks = [(N_META + i * BLOCK_Q, N_META + (i + 1) * BLOCK_Q)
                   for i in range(seq // BLOCK_Q)]

    for l in range(DEPTH):
        last = l == DEPTH - 1
        q_start = N_META if last else 0
        blocks = ([] if last else [(0, N_META)]) + real_blocks
        lam_init = lambda_init_for(l)

        u = rms_norm(h, norm_w[l])
        proj = jnp.einsum('btd,de->bte', u, w_in[l])
        sb_q, sb_k, sb_v, sb_g, df_q, df_k, df_v, df_g = jnp.split(proj, split_at, axis=-1)

        def heads(t, n, d):
            return t.reshape(b, t.shape[1], n, d).transpose(0, 2, 1, 3).astype(f32)

        sbq = heads(sb_q[:, q_start:], SB_HEADS, SB_HEAD_DIM)
        sbk = heads(sb_k, SB_HEADS, SB_HEAD_DIM)
        sbv = heads(sb_v, SB_HEADS, SB_HEAD_DIM)

        dq = df_q.reshape(b, t_all, DIFF_HEADS, 2, DIFF_QK_DIM).astype(f32)
        dk = df_k.reshape(b, t_all, DIFF_HEADS, 2, DIFF_QK_DIM).astype(f32)
        dq = partial_rope(rms_norm(dq, q_norm_w[l]), cos, sin).transpose(0, 2, 3, 1, 4)
        dk = partial_rope(rms_norm(dk, k_norm_w[l]), cos, sin).transpose(0, 2, 3, 1, 4)
        dq = dq[:, :, :, q_start:]
        dv = heads(df_v, DIFF_HEADS, DIFF_V_DIM)
        lam = (jnp.exp(jnp.sum(lambda_q1[l].astype(f32) * lambda_k1[l].astype(f32)))
               - jnp.exp(jnp.sum(lambda_q2[l].astype(f32) * lambda_k2[l].astype(f32)))
               + lam_init)

        sb_outs, df_outs = [], []
        for (a, e) in blocks:
            q_pos = jnp.arange(a, e)
            k_pos = jnp.arange(e)
            qa, qe = a - q_start, e - q_start
            sb_outs.append(stick_breaking_block(
                sbq[:, :, qa:qe], sbk[:, :, :e], sbv[:, :, :e], q_pos, k_pos))
            df_outs.append(diff_attention_block(
                dq[:, :, :, qa:qe], dk[:, :, :, :e], dv[:, :, :e], lam, q_pos, k_pos))

        t_q = t_all - q_start
        sb_o = jnp.concatenate(sb_outs, axis=2).transpose(0, 2, 1, 3).reshape(b, t_q, SB_WIDTH)
        df_o = jnp.concatenate(df_outs, axis=2).transpose(0, 2, 1, 3)
        df_o = (rms_norm(df_o, subln_w[l], SUBLN_EPS) * (1.0 - lam_init)).reshape(b, t_q, DIFF_WIDTH)

        mixed = jnp.concatenate(
            [sb_o * jax.nn.silu(sb_g[:, q_start:].astype(f32)),
             df_o * jax.nn.silu(df_g[:, q_start:].astype(f32))], axis=-1).astype(h.dtype)
        y = jnp.einsum('bte,ed->btd', mixed, w_out[l])
        h = jnp.concatenate([h[:, :q_start], h[:, q_start:] + y], axis=1)

    return h[:, N_META:]
```

```python
import math
from contextlib import ExitStack
import numpy as np
import concourse.bass as bass
import concourse.mybir as mybir
from concourse.bass_utils import run_bass_kernel_spmd

F32 = mybir.dt.float32
BF16 = mybir.dt.bfloat16
AF = mybir.ActivationFunctionType
ALU = mybir.AluOpType
AX = mybir.AxisListType

D_MODEL = 2048
N_META = 16
NCH = D_MODEL // 128
RMS_EPS = 1e-6
SUBLN_EPS = 1e-5
ROPE_THETA = 500000.0
LAM_INIT = 0.8 - 0.6 * math.exp(-0.3 * 0)
NEG = -30000.0
SB_SCALE = 1.0 / math.sqrt(128.0)
DF_SCALE = 1.0 / math.sqrt(64.0)


class Res:
    __slots__ = ("name", "w", "rs", "acc")

    def __init__(self, name=""):
        self.name = name
        self.w = None
        self.rs = []
        self.acc = {}


class Op:
    __slots__ = ("eng", "fn", "deps", "sem", "val", "needs_inc", "idx")


class Sched:
    ENGS = ("tensor", "vector", "scalar", "gpsimd", "sync")

    def __init__(self, eng_sems):
        self.streams = {e: [] for e in self.ENGS}
        self.eng_sems = eng_sems
        self.n = 0
        self.dma_ops = []

    def add(self, eng, fn, reads=(), writes=(), dma_sem=None, excl=()):
        op = Op()
        op.eng = eng
        op.fn = fn
        op.idx = self.n
        self.n += 1
        deps = []
        for r in excl:
            for e2, o2 in r.acc.items():
                if e2 != eng:
                    deps.append(o2)
            r.acc[eng] = op
        for r in reads:
            if r.w is not None:
                deps.append(r.w)
        for w in writes:
            if w.w is not None:
                deps.append(w.w)
            deps.extend(w.rs)
        op.deps = deps
        op.sem = dma_sem
        op.needs_inc = dma_sem is not None
        op.val = None
        for d in deps:
            d.needs_inc = True
        for r in reads:
            r.rs.append(op)
        for w in writes:
            w.w = op
            w.rs = []
        self.streams[eng].append(op)
        if dma_sem is not None:
            self.dma_ops.append(op)
        return op

    def emit(self, block, dma_counts):
        all_ops = []
        for e in self.ENGS:
            all_ops.extend(self.streams[e])
        all_ops.sort(key=lambda o: o.idx)
        eng_count = {e: 0 for e in self.ENGS}
        for op in all_ops:
            if op.sem is not None:
                k = id(op.sem)
                dma_counts[k] = dma_counts.get(k, 0) + 16
                op.val = dma_counts[k]
            elif op.needs_inc:
                eng_count[op.eng] += 1
                op.val = eng_count[op.eng]
        final = {}
        for op in self.dma_ops:
            final[id(op.sem)] = (op.sem, op.val)
        streams = self.streams
        eng_sems = self.eng_sems

        def run(engname):
            def body(eng):
                waited = {}
                for op in streams[engname]:
                    need = {}
                    for d in op.deps:
                        if d.sem is not None:
                            key = ("d", id(d.sem))
                            h = d.sem
                        else:
                            if d.eng == "tensor" and engname == "tensor":
                                continue
                            key = ("e", d.eng)
                            h = eng_sems[d.eng]
                        if waited.get(key, 0) >= d.val:
                            continue
                        if key not in need or need[key][1] < d.val:
                            need[key] = (h, d.val)
                    for key, (h, v) in need.items():
                        eng.wait_ge(h, v)
                        waited[key] = v
                    ins = op.fn(eng)
                    if op.sem is not None:
                        ins.then_inc(op.sem, 16)
                    elif op.needs_inc:
                        ins.then_inc(eng_sems[engname], 1)
                if engname == "sync":
                    for (h, v) in final.values():
                        eng.wait_ge(h, v)
            return body

        block.tensor(run("tensor"))
        block.vector(run("vector"))
        block.scalar(run("scalar"))
        block.gpsimd(run("gpsimd"))
        block.sync(run("sync"))


def tok_block(b):
    if b == 0:
        return 0, N_META
    return N_META + 128 * (b - 1), 128


def build_program(SEQ, head_units=None):
    T = N_META + SEQ
    NB = 1 + SEQ // 128
    QT_W = 512 if SEQ % 512 == 0 else 128
    NQT = SEQ // QT_W
    QB = QT_W // 128
    if head_units is None:
        head_units = list(range(16))

    nc = bass.Bass("TRN2", target_bir_lowering=False)
    x_d = nc.dram_tensor("x", [SEQ, D_MODEL], F32, kind="ExternalInput").ap()
    meta_d = nc.dram_tensor("meta", [N_META, D_MODEL], F32, kind="ExternalInput").ap()
    nwbc_d = nc.dram_tensor("nw_bc", [128, NCH, 128], F32, kind="ExternalInput").ap()
    win_d = nc.dram_tensor("w_in_r", [16, D_MODEL, 512], F32, kind="ExternalInput").ap()
    wout_d = nc.dram_tensor("w_out", [D_MODEL, D_MODEL], F32, kind="ExternalInput").ap()
    cmat_d = nc.dram_tensor("cmats", [128, 6, 128], F32, kind="ExternalInput").ap()
    rope_d = nc.dram_tensor("rope", [128, 2, NB, 16], F32, kind="ExternalInput").ap()
    par_d = nc.dram_tensor("params", [128, 5, 128], F32, kind="ExternalInput").ap()
    out_d = nc.dram_tensor("out", [SEQ, D_MODEL], F32, kind="ExternalOutput").ap()
    mix_d = nc.dram_tensor("mix_scratch", [SEQ // 128, 128, NCH, 128], BF16, kind="Internal").ap()

    dma_counts = {}
    with ExitStack() as es0:
        def sb0(name, shape, dt):
            return es0.enter_context(nc.sbuf_tensor(name, shape, dt))

        def sem0(name):
            return es0.enter_context(nc.semaphore(name))

        cmat = sb0("cmat", [128, 6, 128], BF16)
        ropet = sb0("ropet", [128, 2, NB, 16], F32)
        par = sb0("par", [128, 5, 128], F32)
        stat = sb0("stat", [128, 4, NB], F32)
        sc = sb0("sc", [128, 16], F32)
        ident = cmat[:, 0, :]
        negU = cmat[:, 1, :]
        negOnes = cmat[:, 2, :]
        ones = cmat[:, 3, :]
        maskS = cmat[:, 4, :]
        maskD = cmat[:, 5, :]
        neg_lam = sc[:, 5:6]
        sublnw_s = sc[:, 6:7]
        negC = sc[:, 9:10]

        eng_sem_sets = []
        for blk in range(3):
            eng_sem_sets.append({e: sem0(f"e{blk}_{e}") for e in ("tensor", "vector", "scalar", "gpsimd")})
        dsems = [sem0(f"dma{i}") for i in range(12)]

        banks = [es0.enter_context(nc.psum_tensor(f"bank{i}", [128, 512], F32)) for i in range(8)]

        with nc.sbuf_tensor("hT", [128, NCH, T], BF16) as hT:
            with ExitStack() as esA:
                nwbc = esA.enter_context(nc.sbuf_tensor("nwbc", [128, NCH, 128], F32))
                xt = [esA.enter_context(nc.sbuf_tensor(f"xtA{i}", [128, D_MODEL], F32)) for i in range(2)]
                xb = [esA.enter_context(nc.sbuf_tensor(f"xbA{i}", [128, D_MODEL], BF16)) for i in range(2)]
                junk = [esA.enter_context(nc.sbuf_tensor(f"junkA{i}", [128, D_MODEL], BF16)) for i in range(2)]
                r_junk = [Res(), Res()]
                tmpA = esA.enter_context(nc.sbuf_tensor("tmpA", [128, 128], F32))
                S = Sched(eng_sem_sets[0])
                r_cmat, r_rope, r_par, r_nwbc = Res(), Res(), Res(), Res()
                S.add("gpsimd", lambda e: e.dma_start(out=cmat[:], in_=cmat_d), writes=[r_cmat], dma_sem=dsems[0])
                S.add("sync", lambda e: e.dma_start(out=ropet[:], in_=rope_d), writes=[r_rope], dma_sem=dsems[1])
                S.add("sync", lambda e: e.dma_start(out=par[:], in_=par_d), writes=[r_par], dma_sem=dsems[2])
                S.add("sync", lambda e: e.dma_start(out=nwbc[:], in_=nwbc_d), writes=[r_nwbc], dma_sem=dsems[3])
                r_stat = Res()
                r_sc = Res()
                S.add("vector", lambda e: e.memset(stat[:], 1.0), writes=[r_stat])
                r_xt = [Res(), Res()]
                r_xb = [Res(), Res()]
                r_tp = [Res(), Res()]
                r_ssb = [Res() for _ in range(NB)]
                for b in range(NB):
                    t0, P = tok_block(b)
                    sl = b % 2
                    src = meta_d if b == 0 else x_d[128 * (b - 1):128 * b, :]
                    S.add("sync", lambda e, sl=sl, P=P, src=src: e.dma_start(out=xt[sl][:P, :], in_=src),
                          writes=[r_xt[sl]], dma_sem=dsems[4 + sl])
                    S.add("scalar", lambda e, sl=sl, P=P, b=b: e.activation(
                        out=junk[sl][:P, :], in_=xt[sl][:P, :], func=AF.Square, accum_out=stat[:P, 0, b:b + 1]),
                        reads=[r_xt[sl], r_stat], writes=[r_ssb[b], r_junk[sl]])
                    S.add("scalar", lambda e, sl=sl, P=P: e.activation(
                        out=xb[sl][:P, :], in_=xt[sl][:P, :], func=AF.Copy),
                        reads=[r_xt[sl]], writes=[r_xb[sl]])
                    for c in range(NCH):
                        bk = banks[2 * sl + (c // 8)]
                        dst = bk[:].bitcast(BF16)[:, (c % 8) * 128:(c % 8) * 128 + P]
                        S.add("tensor", lambda e, dst=dst, sl=sl, P=P, c=c: e.transpose(
                            out=dst, in_=xb[sl][:P, c * 128:(c + 1) * 128], identity=ident[:P, :P]),
                            reads=[r_xb[sl], r_cmat], writes=[r_tp[sl]] if c == 0 else [],
                            )
                        if c != 0:
                            r_tp[sl].w = S.streams["tensor"][-1]
                    for half in range(2):
                        bk = banks[2 * sl + half]
                        srcv = bk[:].bitcast(BF16).rearrange("p (c t) -> p c t", t=128)[:, :, :P]
                        S.add("vector", lambda e, srcv=srcv, half=half, t0=t0, P=P: e.tensor_tensor(
                            out=hT[:, 8 * half:8 * half + 8, t0:t0 + P], in0=srcv,
                            in1=nwbc[:, 8 * half:8 * half + 8, :P], op=ALU.mult),
                            reads=[r_tp[sl], r_nwbc])
                S.add("vector", lambda e: e.tensor_scalar(
                    out=stat[:, 1, :], in0=stat[:, 0, :], scalar1=1.0 / D_MODEL, scalar2=RMS_EPS,
                    op0=ALU.mult, op1=ALU.add), reads=r_ssb + [r_stat], writes=[r_stat])
                S.add("scalar", lambda e: e.activation(out=stat[:, 3, :], in_=stat[:, 1, :], func=AF.Ln),
                      reads=[r_stat], writes=[r_stat])
                S.add("scalar", lambda e: e.activation(out=stat[:, 2, :], in_=stat[:, 3, :], func=AF.Exp, scale=-0.5),
                      reads=[r_stat], writes=[r_stat])
                S.add("vector", lambda e: e.tensor_tensor(out=tmpA[:, 0:64], in0=par[:, 2, 0:64], in1=par[:, 2, 64:128],
                                                         op=ALU.mult), reads=[r_par], writes=[r_sc])
                S.add("vector", lambda e: e.tensor_reduce(out=sc[:, 0:1], in_=tmpA[:, 0:64], op=ALU.add, axis=AX.X),
                      reads=[r_sc], writes=[r_sc])
                S.add("vector", lambda e: e.tensor_tensor(out=tmpA[:, 64:128], in0=par[:, 3, 0:64], in1=par[:, 3, 64:128],
                                                         op=ALU.mult), reads=[r_par, r_sc], writes=[r_sc])
                S.add("vector", lambda e: e.tensor_reduce(out=sc[:, 1:2], in_=tmpA[:, 64:128], op=ALU.add, axis=AX.X),
                      reads=[r_sc], writes=[r_sc])
                S.add("scalar", lambda e: e.activation(out=sc[:, 2:4], in_=sc[:, 0:2], func=AF.Exp),
                      reads=[r_sc], writes=[r_sc])
                S.add("vector", lambda e: e.tensor_tensor(out=sc[:, 4:5], in0=sc[:, 2:3], in1=sc[:, 3:4], op=ALU.subtract),
                      reads=[r_sc], writes=[r_sc])
                S.add("vector", lambda e: e.tensor_scalar(out=sc[:, 5:6], in0=sc[:, 4:5], scalar1=LAM_INIT, scalar2=-1.0,
                                                         op0=ALU.add, op1=ALU.mult), reads=[r_sc], writes=[r_sc])
                S.add("vector", lambda e: e.tensor_scalar(out=sc[:, 6:7], in0=par[:, 4, 0:1], scalar1=1.0 - LAM_INIT,
                                                         scalar2=None, op0=ALU.mult), reads=[r_sc, r_par], writes=[r_sc])
                S.add("vector", lambda e: e.tensor_reduce(out=sc[:, 7:9], in_=par[:, 0:2, 0:64], op=ALU.max, axis=AX.X,
                                                         apply_absolute_value=True), reads=[r_sc, r_par], writes=[r_sc])
                S.add("vector", lambda e: e.tensor_tensor(out=sc[:, 9:10], in0=sc[:, 7:8], in1=sc[:, 8:9], op=ALU.mult),
                      reads=[r_sc], writes=[r_sc])
                S.add("vector", lambda e: e.tensor_scalar(out=sc[:, 9:10], in0=sc[:, 9:10], scalar1=-8.0, scalar2=None,
                                                         op0=ALU.mult), reads=[r_sc], writes=[r_sc])
                with nc.Block() as blkA:
                    S.emit(blkA, dma_counts)

            with ExitStack() as esB:
                def sbB(name, shape, dt):
                    return esB.enter_context(nc.sbuf_tensor(name, shape, dt))
                Wb = sbB("Wb", [128, NCH, 512], BF16)
                QKG = sbB("QKG", [128, 3, T], BF16)
                V = sbB("V", [128, NB, 128], BF16)
                rt = [sbB(f"rt_{i}", [128, 2, 4, 16], F32) for i in range(2)]
                s4 = [sbB(f"s4_{i}", [128, 16], F32) for i in range(2)]
                tokb = [sbB(f"tokb_{i}", [128, 384], BF16) for i in range(2)]
                Ebuf = [sbB(f"E_{i}", [128, 512], F32) for i in range(2)]
                Lp = [sbB(f"Lp_{i}", [128, 512], BF16) for i in range(2)]
                Abuf = [sbB(f"A_{i}", [128, 2, 512], BF16) for i in range(2)]
                Lrun = [Abuf[i][:, 1, :] for i in range(2)]
                ep = [sbB(f"ep_{i}", [128, 512], F32) for i in range(3)]
                sqb = Lp[0]
                qk32 = [ep[0][:, 256 * i:256 * i + 256] for i in range(2)]
                sq = [ep[1][:, 256 * i:256 * i + 256] for i in range(2)]
                qkn = [ep[2][:, 256 * i:256 * i + 256] for i in range(2)]
                mixo = [sbB(f"mixo_{i}", [128, 512], BF16) for i in range(2)]

                S = Sched(eng_sem_sets[1])
                r_W = Res()
                r_QKG = Res()
                r_V = Res()
                r_pp = [Res(), Res()]
                r_tq = [Res(), Res()]
                r_qk32 = [Res(), Res()]
                r_sq = [Res(), Res()]
                r_qkn = [Res(), Res()]
                r_rt = [Res(), Res()]
                r_s4 = [Res(), Res()]
                r_tokb = [Res(), Res()]
                r_E = [Res(), Res()]
                r_Lp = [Res(), Res()]
                r_Lrun = [Res(), Res()]
                r_A = [Res(), Res()]
                r_ep = [Res() for _ in range(3)]
                r_sqb = r_Lp[0]
                r_fence = Res()
                r_Epool = Res()
                r_tokg = [Res(), Res()]
                r_mixo = [Res(), Res()]
                r_bank = [Res() for _ in range(8)]
                blk_ctr = [0]
                mix_ctr = [0]

                def load_W(hu):
                    for g in range(4):
                        srcw = win_d[hu, 512 * g:512 * (g + 1), :].rearrange("(c p) n -> p c n", p=128)
                        S.add("gpsimd", lambda e, g=g, srcw=srcw: e.dma_start(out=Wb[:, 4 * g:4 * g + 4, :], in_=srcw),
                              writes=[r_W] if g == 0 else [], dma_sem=dsems[0])
                        if g != 0:
                            r_W.w = S.streams["gpsimd"][-1]

                def inproj(hu):
                    is_df = hu >= 8
                    qkg_writers = []
                    v_writers = []
                    S.add("vector", lambda e: e.memset(sc[:, 14:15], 0.0),
                          writes=[r_QKG, r_V, r_ep[0], r_ep[1], r_ep[2], r_fence])

                    def transposes(b, sl):
                        t0, P = tok_block(b)
                        tqb = banks[2 + sl]
                        tqv = tqb[:].bitcast(BF16)
                        for i in range(3):
                            op = S.add("tensor", lambda e, tqv=tqv, i=i, sl=sl, P=P: e.transpose(
                                out=tqv[:, i * 128:i * 128 + P], in_=tokb[sl][:P, i * 128:(i + 1) * 128],
                                identity=ident[:P, :P]),
                                reads=[r_tokb[sl], r_tokg[sl]], writes=[r_bank[2 + sl]] if i == 0 else [])
                            if i != 0:
                                r_bank[2 + sl].w = op
                        tq3 = tqv[:, 0:384].rearrange("p (a t) -> p a t", t=128)
                        if not is_df:
                            op = S.add("vector", lambda e, tq3=tq3, t0=t0, P=P: e.tensor_scalar(
                                out=QKG[:, 0, t0:t0 + P], in0=tq3[:, 0, :P], scalar1=SB_SCALE, scalar2=None, op0=ALU.mult),
                                reads=[r_bank[2 + sl], r_QKG], excl=[r_bank[2 + sl]])
                            qkg_writers.append(op)
                            op = S.add("vector", lambda e, tq3=tq3, t0=t0, P=P: e.tensor_copy(
                                out=QKG[:, 1:3, t0:t0 + P], in_=tq3[:, 1:3, :P]),
                                reads=[r_bank[2 + sl], r_QKG], excl=[r_bank[2 + sl]])
                            qkg_writers.append(op)
                        else:
                            op = S.add("vector", lambda e, tq3=tq3, t0=t0, P=P: e.tensor_copy(
                                out=QKG[:, 0:3, t0:t0 + P], in_=tq3[:, 0:3, :P]),
                                reads=[r_bank[2 + sl], r_QKG], excl=[r_bank[2 + sl]])
                            qkg_writers.append(op)

                    prev = None
                    for b in range(NB):
                        t0, P = tok_block(b)
                        sl = blk_ctr[0] % 2
                        blk_ctr[0] += 1
                        ppb = banks[sl]
                        for c in range(NCH):
                            op = S.add("tensor", lambda e, ppb=ppb, c=c, t0=t0, P=P: e.matmul(
                                ppb[:P, :], lhsT=hT[:, c, t0:t0 + P], rhs=Wb[:, c, :], start=(c == 0), stop=(c == NCH - 1)),
                                reads=[r_W], writes=[r_bank[sl]] if c == 0 else [])
                            if c != 0:
                                r_bank[sl].w = op
                        rs_b = stat[:P, 2, b:b + 1]
                        if not is_df:
                            S.add("vector", lambda e, ppb=ppb, sl=sl, P=P, rs_b=rs_b: e.tensor_scalar(
                                out=tokb[sl][:P, 0:256], in0=ppb[:P, 0:256], scalar1=rs_b, scalar2=None, op0=ALU.mult),
                                reads=[r_bank[sl]], writes=[r_tokb[sl]], excl=[r_bank[sl]])
                        else:
                            S.add("vector", lambda e, ppb=ppb, sl=sl, P=P, rs_b=rs_b: e.scalar_tensor_tensor(
                                out=qk32[sl][:P, :], in0=ppb[:P, 0:256], scalar=rs_b,
                                in1=par[:P, 0:2, :].rearrange("p a b -> p (a b)"), op0=ALU.mult, op1=ALU.mult),
                                reads=[r_bank[sl], r_fence], writes=[r_qk32[sl]], excl=[r_bank[sl]])
                        op = S.add("vector", lambda e, ppb=ppb, b=b, P=P, rs_b=rs_b: e.tensor_scalar(
                            out=V[:P, b, :], in0=ppb[:P, 256:384], scalar1=rs_b, scalar2=None, op0=ALU.mult),
                            reads=[r_bank[sl], r_V], excl=[r_bank[sl]])
                        v_writers.append(op)
                        S.add("scalar", lambda e, ppb=ppb, sl=sl, P=P, rs_b=rs_b: e.activation(
                            out=tokb[sl][:P, 256:384], in_=ppb[:P, 384:512], func=AF.Copy, scale=rs_b),
                            reads=[r_bank[sl]], writes=[r_tokg[sl]], excl=[r_bank[sl]])
                        if is_df:
                            S.add("scalar", lambda e, ppb=ppb, sl=sl, P=P, rs_b=rs_b: e.activation(
                                out=sq[sl][:P, :], in_=ppb[:P, 0:256], func=AF.Square, scale=rs_b),
                                reads=[r_bank[sl], r_fence], writes=[r_sq[sl]], excl=[r_bank[sl]])
                            qv = qk32[sl][:P, :].rearrange("p (a b) -> p a b", b=64)
                            ccb = ropet[:P, 0, b, :].unsqueeze(1).broadcast_to([P, 4, 16])
                            nsb = ropet[:P, 1, b, 0:8].unsqueeze(1).broadcast_to([P, 4, 8])
                            psb = ropet[:P, 1, b, 8:16].unsqueeze(1).broadcast_to([P, 4, 8])
                            S.add("gpsimd", lambda e, sl=sl, P=P, qv=qv, ccb=ccb: e.tensor_tensor(
                                out=rt[sl][:P, 0, :, :], in0=qv[:, :, 0:16], in1=ccb, op=ALU.mult),
                                reads=[r_qk32[sl]], writes=[r_rt[sl]])
                            S.add("gpsimd", lambda e, sl=sl, P=P, qv=qv, nsb=nsb: e.tensor_tensor(
                                out=rt[sl][:P, 1, :, 0:8], in0=qv[:, :, 8:16], in1=nsb, op=ALU.mult),
                                reads=[r_qk32[sl], r_rt[sl]], writes=[r_rt[sl]])
                            S.add("gpsimd", lambda e, sl=sl, P=P, qv=qv, psb=psb: e.tensor_tensor(
                                out=rt[sl][:P, 1, :, 8:16], in0=qv[:, :, 0:8], in1=psb, op=ALU.mult),
                                reads=[r_qk32[sl], r_rt[sl]], writes=[r_rt[sl]])
                            S.add("gpsimd", lambda e, sl=sl, P=P: e.tensor_tensor(
                                out=rt[sl][:P, 0, :, :], in0=rt[sl][:P, 0, :, :], in1=rt[sl][:P, 1, :, :], op=ALU.add),
                                reads=[r_rt[sl]], writes=[r_rt[sl]])
                            S.add("vector", lambda e, sl=sl, P=P: e.tensor_reduce(
                                out=s4[sl][:P, 0:4], in_=sq[sl][:P, :].rearrange("p (a b) -> p a b", b=64),
                                op=ALU.add, axis=AX.X), reads=[r_sq[sl]], writes=[r_s4[sl]])
                            S.add("vector", lambda e, sl=sl, P=P: e.tensor_scalar(
                                out=s4[sl][:P, 4:8], in0=s4[sl][:P, 0:4], scalar1=1.0 / 64.0, scalar2=RMS_EPS,
                                op0=ALU.mult, op1=ALU.add), reads=[r_s4[sl]], writes=[r_s4[sl]])
                            S.add("scalar", lambda e, sl=sl, P=P: e.activation(
                                out=s4[sl][:P, 8:12], in_=s4[sl][:P, 4:8], func=AF.Ln), reads=[r_s4[sl]], writes=[r_s4[sl]])
                            S.add("scalar", lambda e, sl=sl, P=P: e.activation(
                                out=s4[sl][:P, 12:16], in_=s4[sl][:P, 8:12], func=AF.Exp, scale=-0.5),
                                reads=[r_s4[sl]], writes=[r_s4[sl]])
                            tv = tokb[sl][:P, 0:256].rearrange("p (a b) -> p a b", b=64)
                            S.add("vector", lambda e, sl=sl, P=P, tv=tv, qv=qv: e.tensor_tensor(
                                out=tv, in0=qv, in1=s4[sl][:P, 12:16].unsqueeze(2).broadcast_to([P, 4, 64]), op=ALU.mult),
                                reads=[r_qk32[sl], r_s4[sl]], writes=[r_tokb[sl]])
                            S.add("vector", lambda e, sl=sl, P=P, tv=tv: e.tensor_tensor(
                                out=tv[:, :, 0:16], in0=rt[sl][:P, 0, :, :],
                                in1=s4[sl][:P, 12:16].unsqueeze(2).broadcast_to([P, 4, 16]), op=ALU.mult),
                                reads=[r_rt[sl], r_s4[sl], r_tokb[sl]], writes=[r_tokb[sl]])
                        if prev is not None:
                            transposes(*prev)
                        prev = (b, sl)
                    transposes(*prev)
                    gq = S.add("vector", lambda e: e.memset(sc[:, 13:14], 0.0), reads=[], writes=[])
                    gq.deps = list(qkg_writers)
                    for d in gq.deps:
                        d.needs_inc = True
                    r_G = Res()
                    r_G.w = gq
                    nseg = 4
                    segw = (T + nseg - 1) // nseg
                    for sg in range(nseg):
                        a0, a1 = sg * segw, min(T, (sg + 1) * segw)
                        op = S.add("scalar", lambda e, a0=a0, a1=a1: e.activation(
                            out=QKG[:, 2, a0:a1], in_=QKG[:, 2, a0:a1], func=AF.Silu), reads=[r_G])
                        qkg_writers.append(op)
                    jq = S.add("vector", lambda e: e.memset(sc[:, 15:16], 0.0), reads=[], writes=[])
                    jq.deps = list(qkg_writers) + list(v_writers)
                    for d in jq.deps:
                        d.needs_inc = True
                    r_QKG.w = jq
                    r_QKG.rs = []
                    r_V.w = jq
                    r_V.rs = []

                def qcols(j, qoff):
                    c0 = N_META + QT_W * j + qoff
                    return c0, N_META + QT_W * (j + 1)

                def store_mix(hu, j, sl):
                    c = hu
                    dst = mix_d[QB * j:QB * (j + 1), :, c, :].rearrange("b p t -> p b t")
                    S.add("sync", lambda e, dst=dst, sl=sl: e.dma_start(
                        out=dst, in_=mixo[sl][:, 0:QT_W].rearrange("p (b t) -> p b t", t=128)),
                        reads=[r_mixo[sl]], dma_sem=dsems[6 + sl])

                def make_items():
                    items = []
                    for j in range(NQT):
                        kbs = list(range(QB * j + QB, -1, -1))
                        for n_, kb in enumerate(kbs):
                            k0, ks = tok_block(kb)
                            diag = kb > QB * j
                            qoff = (kb - (QB * j + 1)) * 128 if diag else 0
                            q0, q1 = qcols(j, qoff)
                            items.append(dict(j=j, kb=kb, k0=k0, ks=ks, diag=diag, qoff=qoff, q0=q0, q1=q1,
                                              first=(n_ == 0), last=(kb == 0)))
                    return items

                sb_tile_ctr = [0]

                def attn_sb(hu):
                    QT_, KT_, GT_ = QKG[:, 0, :], QKG[:, 1, :], QKG[:, 2, :]
                    items = make_items()
                    n = len(items)
                    lcur = [0]
                    obank_of_tile = {}

                    def obank(j):
                        if j not in obank_of_tile:
                            obank_of_tile[j] = (7, 3)[sb_tile_ctr[0] % 2]
                            sb_tile_ctr[0] += 1
                        return obank_of_tile[j]

                    def st_qk(i):
                        it = items[i]
                        zs = 4 + (i % 3)
                        Zb = banks[zs]
                        ks, qoff, k0, q0, q1, diag = it["ks"], it["qoff"], it["k0"], it["q0"], it["q1"], it["diag"]
                        op = S.add("tensor", lambda e: e.matmul(
                            Zb[:ks, qoff:QT_W], lhsT=KT_[:, k0:k0 + ks], rhs=QT_[:, q0:q1], start=True, stop=not diag),
                            reads=[r_QKG], writes=[r_bank[zs]])
                        if diag:
                            op = S.add("tensor", lambda e: e.matmul(
                                Zb[:ks, qoff:qoff + 128], lhsT=ident[:, :ks], rhs=maskS[:, :], start=False, stop=True))
                            r_bank[zs].w = op

                    def st_e1ln(i):
                        it = items[i]
                        zs = 4 + (i % 3)
                        Zb = banks[zs]
                        sl = i % 2
                        ks, qoff = it["ks"], it["qoff"]
                        Eb = Ebuf[sl][:].bitcast(BF16)
                        S.add("scalar", lambda e: e.activation(
                            out=Eb[:ks, qoff:QT_W], in_=Zb[:ks, qoff:QT_W], func=AF.Exp),
                            reads=[r_bank[zs]], writes=[r_E[sl]])
                        S.add("scalar", lambda e: e.activation(
                            out=Lp[sl][:ks, qoff:QT_W], in_=Eb[:ks, qoff:QT_W], func=AF.Ln, bias=1.0),
                            reads=[r_E[sl]], writes=[r_Lp[sl]])

                    def st_ul(i):
                        it = items[i]
                        zs = 4 + (i % 3)
                        Zb = banks[zs]
                        sl = i % 2
                        ks, qoff, first = it["ks"], it["qoff"], it["first"]
                        if first:
                            S.add("vector", lambda e: e.memset(Lrun[0], 0.0), writes=[r_Lrun[0], r_A[0]])
                            S.add("vector", lambda e: e.memset(Lrun[1], 0.0), writes=[r_Lrun[1], r_A[1]])
                            lcur[0] = 0
                        lc = lcur[0]
                        op = S.add("tensor", lambda e: e.matmul(
                            Zb[:ks, qoff:QT_W], lhsT=negU[:ks, :ks], rhs=Lp[sl][:ks, qoff:QT_W], start=False, stop=first,
                            skip_group_check=True),
                            reads=[r_Lp[sl], r_bank[zs]], writes=[r_bank[zs]])
                        if not first:
                            op = S.add("tensor", lambda e: e.matmul(
                                Zb[:ks, qoff:QT_W], lhsT=negOnes[:, :ks], rhs=Lrun[lc][:, qoff:QT_W],
                                start=False, stop=True, skip_group_check=True),
                                reads=[r_Lrun[lc]])
                            r_bank[zs].w = op
                        if not it["last"]:
                            S.add("vector", lambda e: e.tensor_tensor(
                                out=Lrun[1 - lc][:, qoff:QT_W], in0=Lrun[lc][:, qoff:QT_W],
                                in1=Lp[sl][:, qoff:QT_W], op=ALU.add),
                                reads=[r_Lrun[lc], r_Lp[sl]], writes=[r_Lrun[1 - lc]])
                            lcur[0] = 1 - lc

                    def st_e2(i):
                        it = items[i]
                        zs = 4 + (i % 3)
                        Zb = banks[zs]
                        sl = i % 2
                        ks, qoff = it["ks"], it["qoff"]
                        S.add("scalar", lambda e: e.activation(
                            out=Abuf[sl][:ks, 0, qoff:QT_W], in_=Zb[:ks, qoff:QT_W], func=AF.Exp),
                            reads=[r_bank[zs]], writes=[r_A[sl]])

                    def st_av(i):
                        it = items[i]
                        sl = i % 2
                        ks, qoff, kb, first, last, j = it["ks"], it["qoff"], it["kb"], it["first"], it["last"], it["j"]
                        ob = obank(j)
                        op = S.add("tensor", lambda e: e.matmul(
                            banks[ob][:, qoff:QT_W], lhsT=V[:ks, kb, :], rhs=Abuf[sl][:ks, 0, qoff:QT_W],
                            start=first, stop=last, skip_group_check=True),
                            reads=[r_A[sl], r_V], writes=[r_bank[ob]] if first else [])
                        if not first:
                            r_bank[ob].w = op
                        if last:
                            msl = mix_ctr[0] % 2
                            mix_ctr[0] += 1
                            q0, q1 = qcols(j, 0)
                            S.add("vector", lambda e: e.tensor_tensor(
                                out=mixo[msl][:, 0:QT_W], in0=banks[ob][:, 0:QT_W], in1=GT_[:, q0:q1], op=ALU.mult),
                                reads=[r_bank[ob], r_QKG], writes=[r_mixo[msl]], excl=[r_bank[ob]])
                            store_mix(hu, j, msl)

                    for t in range(-2, n + 1):
                        if 0 <= t < n:
                            st_ul(t)
                        if 0 <= t - 1 < n:
                            st_av(t - 1)
                        if 0 <= t + 2 < n:
                            st_qk(t + 2)
                        if 0 <= t + 1 < n:
                            st_e1ln(t + 1)
                        if 0 <= t < n:
                            st_e2(t)

                def attn_df(hu):
                    QT_, KT_, GT_ = QKG[:, 0, :], QKG[:, 1, :], QKG[:, 2, :]
                    items = make_items()
                    n = len(items)
                    oB = [4, 5]
                    sB = [6, 7]
                    W_ = QT_W
                    Pacc = Ebuf[0][:].bitcast(BF16).rearrange("p (c n) -> p c n", c=2)
                    A3 = [Abuf[0], Abuf[1], Ebuf[1][:].bitcast(BF16).rearrange("p (c n) -> p c n", c=2)]
                    r_A3 = [r_A[0], r_A[1], r_E[1]]
                    r_Pacc = [r_E[0], r_E[0]]

                    def st_qk(i):
                        it = items[i]
                        sl = i % 2
                        ks, qoff, k0, q0, q1, diag = it["ks"], it["qoff"], it["k0"], it["q0"], it["q1"], it["diag"]
                        for c in range(2):
                            zs = 2 * sl + c
                            Zb = banks[zs]
                            op = S.add("tensor", lambda e, Zb=Zb, c=c: e.matmul(
                                Zb[:ks, qoff:QT_W], lhsT=KT_[64 * c:64 * c + 64, k0:k0 + ks],
                                rhs=QT_[64 * c:64 * c + 64, q0:q1], start=True, stop=not diag),
                                reads=[r_QKG], writes=[r_bank[zs]])
                            if diag:
                                op = S.add("tensor", lambda e, Zb=Zb: e.matmul(
                                    Zb[:ks, qoff:qoff + 128], lhsT=ident[:, :ks], rhs=maskD[:, :], start=False, stop=True))
                                r_bank[zs].w = op

                    def st_exp(i):
                        asl = i % 3
                        it = items[i]
                        sl = i % 2
                        ks, qoff = it["ks"], it["qoff"]
                        for c in range(2):
                            zs = 2 * sl + c
                            Zb = banks[zs]
                            S.add("scalar", lambda e, Zb=Zb, c=c: e.activation(
                                out=A3[asl][:ks, c, qoff:QT_W], in_=Zb[:ks, qoff:QT_W], func=AF.Exp,
                                scale=DF_SCALE),
                                reads=[r_bank[zs]],
                                writes=([r_A3[asl]] + ([r_Lrun[asl]] if asl < 2 else [])) if c == 0 else [])
                            if c == 1:
                                r_A3[asl].w = S.streams["scalar"][-1]

                    def st_av(i):
                        it = items[i]
                        sl = i % 2
                        asl = i % 3
                        ks, qoff, kb, first, last, j = it["ks"], it["qoff"], it["kb"], it["first"], it["last"], it["j"]
                        for c in range(2):
                            op = S.add("tensor", lambda e, c=c: e.matmul(
                                banks[oB[c]][:, qoff:QT_W], lhsT=V[:ks, kb, :], rhs=A3[asl][:ks, c, qoff:QT_W],
                                start=first, stop=last, skip_group_check=True),
                                reads=[r_A3[asl], r_V], writes=[r_bank[oB[c]]] if first else [])
                            if not first:
                                r_bank[oB[c]].w = op
                        if last:
                            epilogue(j)

                    def st_acc(i):
                        it = items[i]
                        sl = i % 2
                        asl = i % 3
                        ks, qoff, first = it["ks"], it["qoff"], it["first"]
                        if first:
                            S.add("vector", lambda e: e.memset(Pacc[:, :, :], 0.0), writes=[r_Pacc[0]])
                        S.add("vector", lambda e: e.tensor_tensor(
                            out=Pacc[:ks, :, qoff:QT_W], in0=Pacc[:ks, :, qoff:QT_W], in1=A3[asl][:ks, :, qoff:QT_W],
                            op=ALU.add), reads=[r_A3[asl], r_Pacc[0]], writes=[r_Pacc[0]])

                    pending = []

                    def epilogue(j):
                        while pending:
                            pending.pop(0)()
                        q0, q1 = qcols(j, 0)
                        msl = mix_ctr[0] % 2
                        mix_ctr[0] += 1
                        for c in range(2):
                            S.add("tensor", lambda e, c=c: e.matmul(
                                banks[sB[c]][:, 0:QT_W], lhsT=ones[:, :], rhs=Pacc[:, c, 0:QT_W], start=True, stop=True),
                                reads=[r_Pacc[c]], writes=[r_bank[sB[c]]])
                        S.add("vector", lambda e: e.tensor_copy(out=ep[2][:, 0:W_], in_=banks[oB[0]][:, 0:W_]),
                              reads=[r_bank[oB[0]]], writes=[r_ep[2]], excl=[r_bank[oB[0]]])
                        S.add("vector", lambda e: e.tensor_copy(out=ep[1][:, 0:W_], in_=banks[oB[1]][:, 0:W_]),
                              reads=[r_bank[oB[1]]], writes=[r_ep[1]], excl=[r_bank[oB[1]]])

                        def s1():
                            S.add("scalar", lambda e: e.activation(out=ep[0][:, 0:W_], in_=banks[sB[0]][:, 0:W_], func=AF.Ln),
                                  reads=[r_bank[sB[0]]], writes=[r_ep[0]], excl=[r_bank[sB[0]]])
                            S.add("scalar", lambda e: e.activation(out=ep[0][:, 0:W_], in_=ep[0][:, 0:W_], func=AF.Exp, scale=-1.0),
                                  reads=[r_ep[0]], writes=[r_ep[0]])

                        def s2():
                            S.add("vector", lambda e: e.tensor_tensor(
                                out=ep[2][:, 0:W_], in0=ep[2][:, 0:W_], in1=ep[0][:, 0:W_], op=ALU.mult),
                                reads=[r_ep[2], r_ep[0]], writes=[r_ep[2]])

                        def s3():
                            S.add("scalar", lambda e: e.activation(out=ep[0][:, 0:W_], in_=banks[sB[1]][:, 0:W_], func=AF.Ln),
                                  reads=[r_bank[sB[1]]], writes=[r_ep[0]], excl=[r_bank[sB[1]]])
                            S.add("scalar", lambda e: e.activation(out=ep[0][:, 0:W_], in_=ep[0][:, 0:W_], func=AF.Exp, scale=-1.0),
                                  reads=[r_ep[0]], writes=[r_ep[0]])

                        def s4_():
                            S.add("vector", lambda e: e.tensor_tensor(
                                out=ep[1][:, 0:W_], in0=ep[1][:, 0:W_], in1=ep[0][:, 0:W_], op=ALU.mult),
                                reads=[r_ep[1], r_ep[0]], writes=[r_ep[1]])
                            S.add("vector", lambda e: e.scalar_tensor_tensor(
                                out=ep[2][:, 0:W_], in0=ep[1][:, 0:W_], scalar=neg_lam, in1=ep[2][:, 0:W_],
                                op0=ALU.mult, op1=ALU.add), reads=[r_ep[1], r_ep[2]], writes=[r_ep[2]])

                        def s5():
                            S.add("scalar", lambda e: e.activation(out=sqb[:, 0:W_], in_=ep[2][:, 0:W_], func=AF.Square),
                                  reads=[r_ep[2]], writes=[r_sqb])
                            S.add("tensor", lambda e: e.matmul(banks[sB[0]][:, 0:W_], lhsT=ones[:, :], rhs=sqb[:, 0:W_],
                                                               start=True, stop=True),
                                  reads=[r_sqb], writes=[r_bank[sB[0]]])

                        def s6():
                            S.add("vector", lambda e: e.tensor_scalar(
                                out=ep[0][:, 0:W_], in0=banks[sB[0]][:, 0:W_], scalar1=1.0 / 128.0, scalar2=SUBLN_EPS,
                                op0=ALU.mult, op1=ALU.add), reads=[r_bank[sB[0]]], writes=[r_ep[0]], excl=[r_bank[sB[0]]])
                            S.add("scalar", lambda e: e.activation(out=ep[1][:, 0:W_], in_=ep[0][:, 0:W_], func=AF.Ln),
                                  reads=[r_ep[0]], writes=[r_ep[1]])
                            S.add("scalar", lambda e: e.activation(out=ep[0][:, 0:W_], in_=ep[1][:, 0:W_], func=AF.Exp, scale=-0.5),
                                  reads=[r_ep[1]], writes=[r_ep[0]])

                        def s7():
                            S.add("vector", lambda e: e.tensor_tensor(
                                out=ep[1][:, 0:W_], in0=ep[2][:, 0:W_], in1=ep[0][:, 0:W_], op=ALU.mult),
                                reads=[r_ep[2], r_ep[0]], writes=[r_ep[1]])
                            S.add("vector", lambda e: e.scalar_tensor_tensor(
                                out=mixo[msl][:, 0:W_], in0=ep[1][:, 0:W_], scalar=sublnw_s, in1=GT_[:, q0:q1],
                                op0=ALU.mult, op1=ALU.mult), reads=[r_ep[1], r_QKG], writes=[r_mixo[msl]])
                            store_mix(hu, j, msl)

                        pending.extend([s1, s2, s3, s4_, s5, s6, s7])

                    for t in range(-1, n):
                        if 0 <= t + 1 < n:
                            st_qk(t + 1)
                        had_pending = len(pending) > 0
                        if 0 <= t < n:
                            st_av(t)
                        if 0 <= t + 1 < n:
                            st_exp(t + 1)
                            st_acc(t + 1)
                        if had_pending and (t % 2 == 0):
                            pending.pop(0)()
                    while pending:
                        pending.pop(0)()

                for ih, hu in enumerate(head_units):
                    if ih == 0:
                        load_W(hu)
                    inproj(hu)
                    if ih + 1 < len(head_units):
                        load_W(head_units[ih + 1])
                    if hu < 8:
                        attn_sb(hu)
                    else:
                        attn_df(hu)
                with nc.Block() as blkB:
                    S.emit(blkB, dma_counts)

        with ExitStack() as esC:
            def sbC(name, shape, dt):
                return esC.enter_context(nc.sbuf_tensor(name, shape, dt))
            Wo = sbC("Wo", [128, NCH, D_MODEL], BF16)
            mT = [sbC(f"mT_{i}", [128, NCH, 128], BF16) for i in range(2)]
            xr = [sbC(f"xr_{i}", [128, D_MODEL], F32) for i in range(2)]
            yo = [sbC(f"yo_{i}", [128, D_MODEL], F32) for i in range(2)]
            S = Sched(eng_sem_sets[2])
            r_Wo = Res()
            r_mT = [Res(), Res()]
            r_xr = [Res(), Res()]
            r_yo = [Res(), Res()]
            r_bank = [Res() for _ in range(8)]
            r_Woh = [Res(), Res()]
            wo_sems = [dsems[0], dsems[8]]
            for hh in range(2):
                for c in range(NCH):
                    srcw = wout_d[128 * c:128 * (c + 1), 1024 * hh:1024 * (hh + 1)]
                    S.add("gpsimd", lambda e, c=c, hh=hh, srcw=srcw: e.dma_start(
                        out=Wo[:, c, 1024 * hh:1024 * (hh + 1)], in_=srcw),
                        writes=[r_Woh[hh]] if c == 0 else [], dma_sem=wo_sems[hh])
                    if c != 0:
                        r_Woh[hh].w = S.streams["gpsimd"][-1]
            bc = 0
            for tb in range(SEQ // 128):
                sl = tb % 2
                S.add("sync", lambda e, sl=sl, tb=tb: e.dma_start(out=mT[sl][:], in_=mix_d[tb]),
                      writes=[r_mT[sl]], dma_sem=dsems[1 + sl])
                S.add("sync", lambda e, sl=sl, tb=tb: e.dma_start(out=xr[sl][:], in_=x_d[128 * tb:128 * (tb + 1), :]),
                      writes=[r_xr[sl]], dma_sem=dsems[3 + sl])
                for n in range(D_MODEL // 512):
                    bk = bc % 4
                    bc += 1
                    for c in range(NCH):
                        op = S.add("tensor", lambda e, bk=bk, sl=sl, c=c, n=n: e.matmul(
                            banks[bk][:, :], lhsT=mT[sl][:, c, :], rhs=Wo[:, c, 512 * n:512 * (n + 1)],
                            start=(c == 0), stop=(c == NCH - 1)),
                            reads=[r_mT[sl], r_Woh[n // 2]], writes=[r_bank[bk]] if c == 0 else [])
                        if c != 0:
                            r_bank[bk].w = op
                    op = S.add("vector", lambda e, bk=bk, sl=sl, n=n: e.tensor_tensor(
                        out=yo[sl][:, 512 * n:512 * (n + 1)], in0=banks[bk][:, :], in1=xr[sl][:, 512 * n:512 * (n + 1)],
                        op=ALU.add), reads=[r_bank[bk], r_xr[sl]], writes=[r_yo[sl]] if n == 0 else [])
                    if n != 0:
                        r_yo[sl].w = op
                S.add("scalar", lambda e, sl=sl, tb=tb: e.dma_start(out=out_d[128 * tb:128 * (tb + 1), :], in_=yo[sl][:]),
                      reads=[r_yo[sl]], dma_sem=dsems[5 + sl])
            with nc.Block() as blkC:
                S.emit(blkC, dma_counts)
    return nc


def host_constants(SEQ):
    NB = 1 + SEQ // 128
    cm = np.zeros((128, 6, 128), np.float32)
    i = np.arange(128)
    cm[:, 0, :] = np.eye(128, dtype=np.float32)
    cm[:, 1, :] = -(i[:, None] >= i[None, :]).astype(np.float32)
    cm[:, 2, :] = -1.0
    cm[:, 3, :] = 1.0
    cm[:, 4, :] = np.where(i[:, None] >= i[None, :], NEG, 0.0)
    cm[:, 5, :] = np.where(i[:, None] > i[None, :], NEG, 0.0)
    rope = np.zeros((128, 2, NB, 16), np.float32)
    inv_freq = (np.float32(ROPE_THETA) ** (-np.arange(0, 16, 2, dtype=np.float32) / np.float32(16))).astype(np.float32)
    for b in range(NB):
        t0, P = tok_block(b)
        pos = (t0 + np.arange(P)).astype(np.float32)
        ang = (pos[:, None] * inv_freq[None, :]).astype(np.float32)
        co, si = np.cos(ang), np.sin(ang)
        rope[:P, 0, b, 0:8] = co
        rope[:P, 0, b, 8:16] = co
        rope[:P, 1, b, 0:8] = -si
        rope[:P, 1, b, 8:16] = si
    return cm, rope


def kernel(x, meta, norm_w, w_in, q_norm_w, k_norm_w, lambda_q1, lambda_k1, lambda_q2, lambda_k2,
           subln_w, w_out, _head_units=None):
    x = np.asarray(x, np.float32)
    B, SEQ, _ = x.shape
    meta = np.ascontiguousarray(np.asarray(meta, np.float32))
    norm_w = np.asarray(norm_w, np.float32)[0]
    w_in = np.asarray(w_in, np.float32)[0]
    w_out = np.ascontiguousarray(np.asarray(w_out, np.float32)[0])
    cols = []
    for hu in range(16):
        base = 0 if hu < 8 else 4096
        h = hu % 8
        for part in range(4):
            cols.append(np.arange(base + part * 1024 + h * 128, base + part * 1024 + (h + 1) * 128))
    cols = np.concatenate(cols)
    w_in_r = np.ascontiguousarray(w_in[:, cols].reshape(D_MODEL, 16, 512).transpose(1, 0, 2))
    nw_bc = np.ascontiguousarray(np.broadcast_to(norm_w.reshape(NCH, 128).T[:, :, None], (128, NCH, 128))).astype(np.float32)
    params = np.zeros((128, 5, 128), np.float32)
    params[:, 0, :] = np.tile(np.asarray(q_norm_w, np.float32)[0], 2)[None, :]
    params[:, 1, :] = np.tile(np.asarray(k_norm_w, np.float32)[0], 2)[None, :]
    params[:, 2, :64] = np.asarray(lambda_q1, np.float32)[0][None, :]
    params[:, 2, 64:] = np.asarray(lambda_k1, np.float32)[0][None, :]
    params[:, 3, :64] = np.asarray(lambda_q2, np.float32)[0][None, :]
    params[:, 3, 64:] = np.asarray(lambda_k2, np.float32)[0][None, :]
    params[:, 4, 0] = np.asarray(subln_w, np.float32)[0]
    cm, rope = host_constants(SEQ)

    nc = build_program(SEQ, _head_units)
    in_maps = []
    for b in range(B):
        in_maps.append({
            "x": np.ascontiguousarray(x[b]), "meta": meta, "nw_bc": nw_bc, "w_in_r": w_in_r, "w_out": w_out,
            "cmats": cm, "rope": rope, "params": params,
        })
    res = run_bass_kernel_spmd(nc, in_maps, core_ids=list(range(B)))
    out = np.stack([np.asarray(r["out"], np.float32).reshape(SEQ, D_MODEL) for r in res.results], axis=0)
    return out
```

```python
import math
from contextlib import ExitStack
import numpy as np
import concourse.bass as bass
import concourse.mybir as mybir
from concourse.bass_utils import run_bass_kernel_spmd

F32 = mybir.dt.float32
BF16 = mybir.dt.bfloat16
AF = mybir.ActivationFunctionType
ALU = mybir.AluOpType
AX = mybir.AxisListType

D_MODEL = 2048
N_META = 16
NCH = D_MODEL // 128
RMS_EPS = 1e-6
SUBLN_EPS = 1e-5
ROPE_THETA = 500000.0
LAM_INIT = 0.8 - 0.6 * math.exp(-0.3 * 0)
NEG = -30000.0
SB_SCALE = 1.0 / math.sqrt(128.0)
DF_SCALE = 1.0 / math.sqrt(64.0)


class Res:
    __slots__ = ("name", "w", "rs", "acc")

    def __init__(self, name=""):
        self.name = name
        self.w = None
        self.rs = []
        self.acc = {}


class Op:
    __slots__ = ("eng", "fn", "deps", "sem", "val", "needs_inc", "idx")


class Sched:
    ENGS = ("tensor", "vector", "scalar", "gpsimd", "sync")

    def __init__(self, eng_sems):
        self.streams = {e: [] for e in self.ENGS}
        self.eng_sems = eng_sems
        self.n = 0
        self.dma_ops = []

    def add(self, eng, fn, reads=(), writes=(), dma_sem=None, excl=()):
        op = Op()
        op.eng = eng
        op.fn = fn
        op.idx = self.n
        self.n += 1
        deps = []
        for r in excl:
            for e2, o2 in r.acc.items():
                if e2 != eng:
                    deps.append(o2)
            r.acc[eng] = op
        for r in reads:
            if r.w is not None:
                deps.append(r.w)
        for w in writes:
            if w.w is not None:
                deps.append(w.w)
            deps.extend(w.rs)
        op.deps = deps
        op.sem = dma_sem
        op.needs_inc = dma_sem is not None
        op.val = None
        for d in deps:
            d.needs_inc = True
        for r in reads:
            r.rs.append(op)
        for w in writes:
            w.w = op
            w.rs = []
        self.streams[eng].append(op)
        if dma_sem is not None:
            self.dma_ops.append(op)
        return op

    def emit(self, block, dma_counts):
        all_ops = []
        for e in self.ENGS:
            all_ops.extend(self.streams[e])
        all_ops.sort(key=lambda o: o.idx)
        eng_count = {e: 0 for e in self.ENGS}
        for op in all_ops:
            if op.sem is not None:
                k = id(op.sem)
                dma_counts[k] = dma_counts.get(k, 0) + 16
                op.val = dma_counts[k]
            elif op.needs_inc:
                eng_count[op.eng] += 1
                op.val = eng_count[op.eng]
        final = {}
        for op in self.dma_ops:
            final[id(op.sem)] = (op.sem, op.val)
        streams = self.streams
        eng_sems = self.eng_sems

        def run(engname):
            def body(eng):
                waited = {}
                for op in streams[engname]:
                    need = {}
                    for d in op.deps:
                        if d.sem is not None:
                            key = ("d", id(d.sem))
                            h = d.sem
                        else:
                            if d.eng == "tensor" and engname == "tensor":
                                continue
                            key = ("e", d.eng)
                            h = eng_sems[d.eng]
                        if waited.get(key, 0) >= d.val:
                            continue
                        if key not in need or need[key][1] < d.val:
                            need[key] = (h, d.val)
                    for key, (h, v) in need.items():
                        eng.wait_ge(h, v)
                        waited[key] = v
                    ins = op.fn(eng)
                    if op.sem is not None:
                        ins.then_inc(op.sem, 16)
                    elif op.needs_inc:
                        ins.then_inc(eng_sems[engname], 1)
                if engname == "sync":
                    for (h, v) in final.values():
                        eng.wait_ge(h, v)
            return body

        block.tensor(run("tensor"))
        block.vector(run("vector"))
        block.scalar(run("scalar"))
        block.gpsimd(run("gpsimd"))
        block.sync(run("sync"))


def tok_block(b):
    if b == 0:
        return 0, N_META
    return N_META + 128 * (b - 1), 128


def build_program(SEQ, head_units=None):
    T = N_META + SEQ
    NB = 1 + SEQ // 128
    QT_W = 512 if SEQ % 512 == 0 else 128
    NQT = SEQ // QT_W
    QB = QT_W // 128
    if head_units is None:
        head_units = list(range(16))

    nc = bass.Bass("TRN2", target_bir_lowering=False)
    x_d = nc.dram_tensor("x", [SEQ, D_MODEL], F32, kind="ExternalInput").ap()
    meta_d = nc.dram_tensor("meta", [N_META, D_MODEL], F32, kind="ExternalInput").ap()
    nwbc_d = nc.dram_tensor("nw_bc", [128, NCH, 128], F32, kind="ExternalInput").ap()
    win_d = nc.dram_tensor("w_in_r", [16, D_MODEL, 512], F32, kind="ExternalInput").ap()
    wout_d = nc.dram_tensor("w_out", [D_MODEL, D_MODEL], F32, kind="ExternalInput").ap()
    cmat_d = nc.dram_tensor("cmats", [128, 6, 128], F32, kind="ExternalInput").ap()
    rope_d = nc.dram_tensor("rope", [128, 2, NB, 16], F32, kind="ExternalInput").ap()
    par_d = nc.dram_tensor("params", [128, 5, 128], F32, kind="ExternalInput").ap()
    out_d = nc.dram_tensor("out", [SEQ, D_MODEL], F32, kind="ExternalOutput").ap()
    mix_d = nc.dram_tensor("mix_scratch", [SEQ // 128, 128, NCH, 128], BF16, kind="Internal").ap()

    dma_counts = {}
    with ExitStack() as es0:
        def sb0(name, shape, dt):
            return es0.enter_context(nc.sbuf_tensor(name, shape, dt))

        def sem0(name):
            return es0.enter_context(nc.semaphore(name))

        cmat = sb0("cmat", [128, 6, 128], BF16)
        ropet = sb0("ropet", [128, 2, NB, 16], F32)
        par = sb0("par", [128, 5, 128], F32)
        stat = sb0("stat", [128, 4, NB], F32)
        sc = sb0("sc", [128, 16], F32)
        ident = cmat[:, 0, :]
        negU = cmat[:, 1, :]
        negOnes = cmat[:, 2, :]
        ones = cmat[:, 3, :]
        maskS = cmat[:, 4, :]
        maskD = cmat[:, 5, :]
        neg_lam = sc[:, 5:6]
        sublnw_s = sc[:, 6:7]
        negC = sc[:, 9:10]

        eng_sem_sets = []
        for blk in range(3):
            eng_sem_sets.append({e: sem0(f"e{blk}_{e}") for e in ("tensor", "vector", "scalar", "gpsimd")})
        dsems = [sem0(f"dma{i}") for i in range(12)]

        banks = [es0.enter_context(nc.psum_tensor(f"bank{i}", [128, 512], F32)) for i in range(8)]

        with nc.sbuf_tensor("hT", [128, NCH, T], BF16) as hT:
            with ExitStack() as esA:
                nwbc = esA.enter_context(nc.sbuf_tensor("nwbc", [128, NCH, 128], F32))
                xt = [esA.enter_context(nc.sbuf_tensor(f"xtA{i}", [128, D_MODEL], F32)) for i in range(2)]
                xb = [esA.enter_context(nc.sbuf_tensor(f"xbA{i}", [128, D_MODEL], BF16)) for i in range(2)]
                junk = [esA.enter_context(nc.sbuf_tensor(f"junkA{i}", [128, D_MODEL], BF16)) for i in range(2)]
                r_junk = [Res(), Res()]
                tmpA = esA.enter_context(nc.sbuf_tensor("tmpA", [128, 128], F32))
                S = Sched(eng_sem_sets[0])
                r_cmat, r_rope, r_par, r_nwbc = Res(), Res(), Res(), Res()
                S.add("gpsimd", lambda e: e.dma_start(out=cmat[:], in_=cmat_d), writes=[r_cmat], dma_sem=dsems[0])
                S.add("sync", lambda e: e.dma_start(out=ropet[:], in_=rope_d), writes=[r_rope], dma_sem=dsems[1])
                S.add("sync", lambda e: e.dma_start(out=par[:], in_=par_d), writes=[r_par], dma_sem=dsems[2])
                S.add("sync", lambda e: e.dma_start(out=nwbc[:], in_=nwbc_d), writes=[r_nwbc], dma_sem=dsems[3])
                r_stat = Res()
                r_sc = Res()
                S.add("vector", lambda e: e.memset(stat[:], 1.0), writes=[r_stat])
                r_xt = [Res(), Res()]
                r_xb = [Res(), Res()]
                r_tp = [Res(), Res()]
                r_ssb = [Res() for _ in range(NB)]
                for b in range(NB):
                    t0, P = tok_block(b)
                    sl = b % 2
                    src = meta_d if b == 0 else x_d[128 * (b - 1):128 * b, :]
                    S.add("sync", lambda e, sl=sl, P=P, src=src: e.dma_start(out=xt[sl][:P, :], in_=src),
                          writes=[r_xt[sl]], dma_sem=dsems[4 + sl])
                    S.add("scalar", lambda e, sl=sl, P=P, b=b: e.activation(
                        out=junk[sl][:P, :], in_=xt[sl][:P, :], func=AF.Square, accum_out=stat[:P, 0, b:b + 1]),
                        reads=[r_xt[sl], r_stat], writes=[r_ssb[b], r_junk[sl]])
                    S.add("scalar", lambda e, sl=sl, P=P: e.activation(
                        out=xb[sl][:P, :], in_=xt[sl][:P, :], func=AF.Copy),
                        reads=[r_xt[sl]], writes=[r_xb[sl]])
                    for c in range(NCH):
                        bk = banks[2 * sl + (c // 8)]
                        dst = bk[:].bitcast(BF16)[:, (c % 8) * 128:(c % 8) * 128 + P]
                        S.add("tensor", lambda e, dst=dst, sl=sl, P=P, c=c: e.transpose(
                            out=dst, in_=xb[sl][:P, c * 128:(c + 1) * 128], identity=ident[:P, :P]),
                            reads=[r_xb[sl], r_cmat], writes=[r_tp[sl]] if c == 0 else [],
                            )
                        if c != 0:
                            r_tp[sl].w = S.streams["tensor"][-1]
                    for half in range(2):
                        bk = banks[2 * sl + half]
                        srcv = bk[:].bitcast(BF16).rearrange("p (c t) -> p c t", t=128)[:, :, :P]
                        S.add("vector", lambda e, srcv=srcv, half=half, t0=t0, P=P: e.tensor_tensor(
                            out=hT[:, 8 * half:8 * half + 8, t0:t0 + P], in0=srcv,
                            in1=nwbc[:, 8 * half:8 * half + 8, :P], op=ALU.mult),
                            reads=[r_tp[sl], r_nwbc])
                S.add("vector", lambda e: e.tensor_scalar(
                    out=stat[:, 1, :], in0=stat[:, 0, :], scalar1=1.0 / D_MODEL, scalar2=RMS_EPS,
                    op0=ALU.mult, op1=ALU.add), reads=r_ssb + [r_stat], writes=[r_stat])
                S.add("scalar", lambda e: e.activation(out=stat[:, 3, :], in_=stat[:, 1, :], func=AF.Ln),
                      reads=[r_stat], writes=[r_stat])
                S.add("scalar", lambda e: e.activation(out=stat[:, 2, :], in_=stat[:, 3, :], func=AF.Exp, scale=-0.5),
                      reads=[r_stat], writes=[r_stat])
                S.add("vector", lambda e: e.tensor_tensor(out=tmpA[:, 0:64], in0=par[:, 2, 0:64], in1=par[:, 2, 64:128],
                                                         op=ALU.mult), reads=[r_par], writes=[r_sc])
                S.add("vector", lambda e: e.tensor_reduce(out=sc[:, 0:1], in_=tmpA[:, 0:64], op=ALU.add, axis=AX.X),
                      reads=[r_sc], writes=[r_sc])
                S.add("vector", lambda e: e.tensor_tensor(out=tmpA[:, 64:128], in0=par[:, 3, 0:64], in1=par[:, 3, 64:128],
                                                         op=ALU.mult), reads=[r_par, r_sc], writes=[r_sc])
                S.add("vector", lambda e: e.tensor_reduce(out=sc[:, 1:2], in_=tmpA[:, 64:128], op=ALU.add, axis=AX.X),
                      reads=[r_sc], writes=[r_sc])
                S.add("scalar", lambda e: e.activation(out=sc[:, 2:4], in_=sc[:, 0:2], func=AF.Exp),
                      reads=[r_sc], writes=[r_sc])
                S.add("vector", lambda e: e.tensor_tensor(out=sc[:, 4:5], in0=sc[:, 2:3], in1=sc[:, 3:4], op=ALU.subtract),
                      reads=[r_sc], writes=[r_sc])
                S.add("vector", lambda e: e.tensor_scalar(out=sc[:, 5:6], in0=sc[:, 4:5], scalar1=LAM_INIT, scalar2=-1.0,
                                                         op0=ALU.add, op1=ALU.mult), reads=[r_sc], writes=[r_sc])
                S.add("vector", lambda e: e.tensor_scalar(out=sc[:, 6:7], in0=par[:, 4, 0:1], scalar1=1.0 - LAM_INIT,
                                                         scalar2=None, op0=ALU.mult), reads=[r_sc, r_par], writes=[r_sc])
                S.add("vector", lambda e: e.tensor_reduce(out=sc[:, 7:9], in_=par[:, 0:2, 0:64], op=ALU.max, axis=AX.X,
                                                         apply_absolute_value=True), reads=[r_sc, r_par], writes=[r_sc])
                S.add("vector", lambda e: e.tensor_tensor(out=sc[:, 9:10], in0=sc[:, 7:8], in1=sc[:, 8:9], op=ALU.mult),
                      reads=[r_sc], writes=[r_sc])
                S.add("vector", lambda e: e.tensor_scalar(out=sc[:, 9:10], in0=sc[:, 9:10], scalar1=-8.0, scalar2=None,
                                                         op0=ALU.mult), reads=[r_sc], writes=[r_sc])
                with nc.Block() as blkA:
                    S.emit(blkA, dma_counts)

            with ExitStack() as esB:
                def sbB(name, shape, dt):
                    return esB.enter_context(nc.sbuf_tensor(name, shape, dt))
                Wb = sbB("Wb", [128, NCH, 512], BF16)
                QKG = sbB("QKG", [128, 3, T], BF16)
                V = sbB("V", [128, NB, 128], BF16)
                rt = [sbB(f"rt_{i}", [128, 2, 4, 16], F32) for i in range(2)]
                s4 = [sbB(f"s4_{i}", [128, 16], F32) for i in range(2)]
                tokb = [sbB(f"tokb_{i}", [128, 384], BF16) for i in range(2)]
                Ebuf = [sbB(f"E_{i}", [128, 512], F32) for i in range(2)]
                Lp = [sbB(f"Lp_{i}", [128, 512], BF16) for i in range(2)]
                Abuf = [sbB(f"A_{i}", [128, 2, 512], BF16) for i in range(2)]
                Lrun = [Abuf[i][:, 1, :] for i in range(2)]
                ep = [sbB(f"ep_{i}", [128, 512], F32) for i in range(3)]
                sqb = Lp[0]
                qk32 = [ep[0][:, 256 * i:256 * i + 256] for i in range(2)]
                sq = [ep[1][:, 256 * i:256 * i + 256] for i in range(2)]
                qkn = [ep[2][:, 256 * i:256 * i + 256] for i in range(2)]
                mixo = [sbB(f"mixo_{i}", [128, 512], BF16) for i in range(2)]

                S = Sched(eng_sem_sets[1])
                r_W = Res()
                r_QKG = Res()
                r_V = Res()
                r_pp = [Res(), Res()]
                r_tq = [Res(), Res()]
                r_qk32 = [Res(), Res()]
                r_sq = [Res(), Res()]
                r_qkn = [Res(), Res()]
                r_rt = [Res(), Res()]
                r_s4 = [Res(), Res()]
                r_tokb = [Res(), Res()]
                r_E = [Res(), Res()]
                r_Lp = [Res(), Res()]
                r_Lrun = [Res(), Res()]
                r_A = [Res(), Res()]
                r_ep = [Res() for _ in range(3)]
                r_sqb = r_Lp[0]
                r_fence = Res()
                r_Epool = Res()
                r_tokg = [Res(), Res()]
                r_mixo = [Res(), Res()]
                r_bank = [Res() for _ in range(8)]
                blk_ctr = [0]
                mix_ctr = [0]

                def load_W(hu):
                    for g in range(4):
                        srcw = win_d[hu, 512 * g:512 * (g + 1), :].rearrange("(c p) n -> p c n", p=128)
                        S.add("gpsimd", lambda e, g=g, srcw=srcw: e.dma_start(out=Wb[:, 4 * g:4 * g + 4, :], in_=srcw),
                              writes=[r_W] if g == 0 else [], dma_sem=dsems[0])
                        if g != 0:
                            r_W.w = S.streams["gpsimd"][-1]

                def inproj(hu):
                    is_df = hu >= 8
                    qkg_writers = []
                    v_writers = []
                    S.add("vector", lambda e: e.memset(sc[:, 14:15], 0.0),
                          writes=[r_QKG, r_V, r_ep[0], r_ep[1], r_ep[2], r_fence])

                    def transposes(b, sl):
                        t0, P = tok_block(b)
                        tqb = banks[2 + sl]
                        tqv = tqb[:].bitcast(BF16)
                        for i in range(3):
                            op = S.add("tensor", lambda e, tqv=tqv, i=i, sl=sl, P=P: e.transpose(
                                out=tqv[:, i * 128:i * 128 + P], in_=tokb[sl][:P, i * 128:(i + 1) * 128],
                                identity=ident[:P, :P]),
                                reads=[r_tokb[sl], r_tokg[sl]], writes=[r_bank[2 + sl]] if i == 0 else [])
                            if i != 0:
                                r_bank[2 + sl].w = op
                        tq3 = tqv[:, 0:384].rearrange("p (a t) -> p a t", t=128)
                        if not is_df:
                            op = S.add("vector", lambda e, tq3=tq3, t0=t0, P=P: e.tensor_scalar(
                                out=QKG[:, 0, t0:t0 + P], in0=tq3[:, 0, :P], scalar1=SB_SCALE, scalar2=None, op0=ALU.mult),
                                reads=[r_bank[2 + sl], r_QKG], excl=[r_bank[2 + sl]])
                            qkg_writers.append(op)
                            op = S.add("vector", lambda e, tq3=tq3, t0=t0, P=P: e.tensor_copy(
                                out=QKG[:, 1:3, t0:t0 + P], in_=tq3[:, 1:3, :P]),
                                reads=[r_bank[2 + sl], r_QKG], excl=[r_bank[2 + sl]])
                            qkg_writers.append(op)
                        else:
                            op = S.add("vector", lambda e, tq3=tq3, t0=t0, P=P: e.tensor_copy(
                                out=QKG[:, 0:3, t0:t0 + P], in_=tq3[:, 0:3, :P]),
                                reads=[r_bank[2 + sl], r_QKG], excl=[r_bank[2 + sl]])
                            qkg_writers.append(op)

                    prev = None
                    for b in range(NB):
                        t0, P = tok_block(b)
                        sl = blk_ctr[0] % 2
                        blk_ctr[0] += 1
                        ppb = banks[sl]
                        for c in range(NCH):
                            op = S.add("tensor", lambda e, ppb=ppb, c=c, t0=t0, P=P: e.matmul(
                                ppb[:P, :], lhsT=hT[:, c, t0:t0 + P], rhs=Wb[:, c, :], start=(c == 0), stop=(c == NCH - 1)),
                                reads=[r_W], writes=[r_bank[sl]] if c == 0 else [])
                            if c != 0:
                                r_bank[sl].w = op
                        rs_b = stat[:P, 2, b:b + 1]
                        if not is_df:
                            S.add("vector", lambda e, ppb=ppb, sl=sl, P=P, rs_b=rs_b: e.tensor_scalar(
                                out=tokb[sl][:P, 0:256], in0=ppb[:P, 0:256], scalar1=rs_b, scalar2=None, op0=ALU.mult),
                                reads=[r_bank[sl]], writes=[r_tokb[sl]], excl=[r_bank[sl]])
                        else:
                            S.add("vector", lambda e, ppb=ppb, sl=sl, P=P, rs_b=rs_b: e.scalar_tensor_tensor(
                                out=qk32[sl][:P, :], in0=ppb[:P, 0:256], scalar=rs_b,
                                in1=par[:P, 0:2, :].rearrange("p a b -> p (a b)"), op0=ALU.mult, op1=ALU.mult),
                                reads=[r_bank[sl], r_fence], writes=[r_qk32[sl]], excl=[r_bank[sl]])
                        op = S.add("vector", lambda e, ppb=ppb, b=b, P=P, rs_b=rs_b: e.tensor_scalar(
                            out=V[:P, b, :], in0=ppb[:P, 256:384], scalar1=rs_b, scalar2=None, op0=ALU.mult),
                            reads=[r_bank[sl], r_V], excl=[r_bank[sl]])
                        v_writers.append(op)
                        S.add("scalar", lambda e, ppb=ppb, sl=sl, P=P, rs_b=rs_b: e.activation(
                            out=tokb[sl][:P, 256:384], in_=ppb[:P, 384:512], func=AF.Copy, scale=rs_b),
                            reads=[r_bank[sl]], writes=[r_tokg[sl]], excl=[r_bank[sl]])
                        if is_df:
                            S.add("scalar", lambda e, ppb=ppb, sl=sl, P=P, rs_b=rs_b: e.activation(
                                out=sq[sl][:P, :], in_=ppb[:P, 0:256], func=AF.Square, scale=rs_b),
                                reads=[r_bank[sl], r_fence], writes=[r_sq[sl]], excl=[r_bank[sl]])
                            qv = qk32[sl][:P, :].rearrange("p (a b) -> p a b", b=64)
                            ccb = ropet[:P, 0, b, :].unsqueeze(1).broadcast_to([P, 4, 16])
                            nsb = ropet[:P, 1, b, 0:8].unsqueeze(1).broadcast_to([P, 4, 8])
                            psb = ropet[:P, 1, b, 8:16].unsqueeze(1).broadcast_to([P, 4, 8])
                            S.add("gpsimd", lambda e, sl=sl, P=P, qv=qv, ccb=ccb: e.tensor_tensor(
                                out=rt[sl][:P, 0, :, :], in0=qv[:, :, 0:16], in1=ccb, op=ALU.mult),
                                reads=[r_qk32[sl]], writes=[r_rt[sl]])
                            S.add("gpsimd", lambda e, sl=sl, P=P, qv=qv, nsb=nsb: e.tensor_tensor(
                                out=rt[sl][:P, 1, :, 0:8], in0=qv[:, :, 8:16], in1=nsb, op=ALU.mult),
                                reads=[r_qk32[sl], r_rt[sl]], writes=[r_rt[sl]])
                            S.add("gpsimd", lambda e, sl=sl, P=P, qv=qv, psb=psb: e.tensor_tensor(
                                out=rt[sl][:P, 1, :, 8:16], in0=qv[:, :, 0:8], in1=psb, op=ALU.mult),
                                reads=[r_qk32[sl], r_rt[sl]], writes=[r_rt[sl]])
                            S.add("gpsimd", lambda e, sl=sl, P=P: e.tensor_tensor(
                                out=rt[sl][:P, 0, :, :], in0=rt[sl][:P, 0, :, :], in1=rt[sl][:P, 1, :, :], op=ALU.add),
                                reads=[r_rt[sl]], writes=[r_rt[sl]])
                            S.add("vector", lambda e, sl=sl, P=P: e.tensor_reduce(
                                out=s4[sl][:P, 0:4], in_=sq[sl][:P, :].rearrange("p (a b) -> p a b", b=64),
                                op=ALU.add, axis=AX.X), reads=[r_sq[sl]], writes=[r_s4[sl]])
                            S.add("vector", lambda e, sl=sl, P=P: e.tensor_scalar(
                                out=s4[sl][:P, 4:8], in0=s4[sl][:P, 0:4], scalar1=1.0 / 64.0, scalar2=RMS_EPS,
                                op0=ALU.mult, op1=ALU.add), reads=[r_s4[sl]], writes=[r_s4[sl]])
                            S.add("scalar", lambda e, sl=sl, P=P: e.activation(
                                out=s4[sl][:P, 8:12], in_=s4[sl][:P, 4:8], func=AF.Ln), reads=[r_s4[sl]], writes=[r_s4[sl]])
                            S.add("scalar", lambda e, sl=sl, P=P: e.activation(
                                out=s4[sl][:P, 12:16], in_=s4[sl][:P, 8:12], func=AF.Exp, scale=-0.5),
                                reads=[r_s4[sl]], writes=[r_s4[sl]])
                            tv = tokb[sl][:P, 0:256].rearrange("p (a b) -> p a b", b=64)
                            S.add("vector", lambda e, sl=sl, P=P, tv=tv, qv=qv: e.tensor_tensor(
                                out=tv, in0=qv, in1=s4[sl][:P, 12:16].unsqueeze(2).broadcast_to([P, 4, 64]), op=ALU.mult),
                                reads=[r_qk32[sl], r_s4[sl]], writes=[r_tokb[sl]])
                            S.add("vector", lambda e, sl=sl, P=P, tv=tv: e.tensor_tensor(
                                out=tv[:, :, 0:16], in0=rt[sl][:P, 0, :, :],
                                in1=s4[sl][:P, 12:16].unsqueeze(2).broadcast_to([P, 4, 16]), op=ALU.mult),
                                reads=[r_rt[sl], r_s4[sl], r_tokb[sl]], writes=[r_tokb[sl]])
                        if prev is not None:
                            transposes(*prev)
                        prev = (b, sl)
                    transposes(*prev)
                    gq = S.add("vector", lambda e: e.memset(sc[:, 13:14], 0.0), reads=[], writes=[])
                    gq.deps = list(qkg_writers)
                    for d in gq.deps:
                        d.needs_inc = True
                    r_G = Res()
                    r_G.w = gq
                    nseg = 4
                    segw = (T + nseg - 1) // nseg
                    for sg in range(nseg):
                        a0, a1 = sg * segw, min(T, (sg + 1) * segw)
                        op = S.add("scalar", lambda e, a0=a0, a1=a1: e.activation(
                            out=QKG[:, 2, a0:a1], in_=QKG[:, 2, a0:a1], func=AF.Silu), reads=[r_G])
                        qkg_writers.append(op)
                    jq = S.add("vector", lambda e: e.memset(sc[:, 15:16], 0.0), reads=[], writes=[])
                    jq.deps = list(qkg_writers) + list(v_writers)
                    for d in jq.deps:
                        d.needs_inc = True
                    r_QKG.w = jq
                    r_QKG.rs = []
                    r_V.w = jq
                    r_V.rs = []

                def qcols(j, qoff):
                    c0 = N_META + QT_W * j + qoff
                    return c0, N_META + QT_W * (j + 1)

                def store_mix(hu, j, sl):
                    c = hu
                    dst = mix_d[QB * j:QB * (j + 1), :, c, :].rearrange("b p t -> p b t")
                    S.add("sync", lambda e, dst=dst, sl=sl: e.dma_start(
                        out=dst, in_=mixo[sl][:, 0:QT_W].rearrange("p (b t) -> p b t", t=128)),
                        reads=[r_mixo[sl]], dma_sem=dsems[6 + sl])

                def make_items():
                    items = []
                    for j in range(NQT):
                        kbs = list(range(QB * j + QB, -1, -1))
                        for n_, kb in enumerate(kbs):
                            k0, ks = tok_block(kb)
                            diag = kb > QB * j
                            qoff = (kb - (QB * j + 1)) * 128 if diag else 0
                            q0, q1 = qcols(j, qoff)
                            items.append(dict(j=j, kb=kb, k0=k0, ks=ks, diag=diag, qoff=qoff, q0=q0, q1=q1,
                                              first=(n_ == 0), last=(kb == 0)))
                    return items

                sb_tile_ctr = [0]

                def attn_sb(hu):
                    QT_, KT_, GT_ = QKG[:, 0, :], QKG[:, 1, :], QKG[:, 2, :]
                    items = make_items()
                    n = len(items)
                    lcur = [0]
                    obank_of_tile = {}

                    def obank(j):
                        if j not in obank_of_tile:
                            obank_of_tile[j] = (7, 3)[sb_tile_ctr[0] % 2]
                            sb_tile_ctr[0] += 1
                        return obank_of_tile[j]

                    def st_qk(i):
                        it = items[i]
                        zs = 4 + (i % 3)
                        Zb = banks[zs]
                        ks, qoff, k0, q0, q1, diag = it["ks"], it["qoff"], it["k0"], it["q0"], it["q1"], it["diag"]
                        op = S.add("tensor", lambda e: e.matmul(
                            Zb[:ks, qoff:QT_W], lhsT=KT_[:, k0:k0 + ks], rhs=QT_[:, q0:q1], start=True, stop=not diag),
                            reads=[r_QKG], writes=[r_bank[zs]])
                        if diag:
                            op = S.add("tensor", lambda e: e.matmul(
                                Zb[:ks, qoff:qoff + 128], lhsT=ident[:, :ks], rhs=maskS[:, :], start=False, stop=True))
                            r_bank[zs].w = op

                    def st_e1ln(i):
                        it = items[i]
                        zs = 4 + (i % 3)
                        Zb = banks[zs]
                        sl = i % 2
                        ks, qoff = it["ks"], it["qoff"]
                        Epb = banks[sl]
                        S.add("scalar", lambda e: e.activation(
                            out=Epb[:ks, qoff:QT_W], in_=Zb[:ks, qoff:QT_W], func=AF.Exp),
                            reads=[r_bank[zs]], writes=[r_bank[sl]])
                        S.add("scalar", lambda e: e.activation(
                            out=Lp[sl][:ks, qoff:QT_W], in_=Epb[:ks, qoff:QT_W], func=AF.Ln, bias=1.0),
                            reads=[r_bank[sl]], writes=[r_Lp[sl]])

                    def st_ul(i):
                        it = items[i]
                        zs = 4 + (i % 3)
                        Zb = banks[zs]
                        sl = i % 2
                        ks, qoff, first = it["ks"], it["qoff"], it["first"]
                        if first:
                            S.add("vector", lambda e: e.memset(Lrun[0], 0.0), writes=[r_Lrun[0], r_A[0]])
                            S.add("vector", lambda e: e.memset(Lrun[1], 0.0), writes=[r_Lrun[1], r_A[1]])
                            lcur[0] = 0
                        lc = lcur[0]
                        op = S.add("tensor", lambda e: e.matmul(
                            Zb[:ks, qoff:QT_W], lhsT=negU[:ks, :ks], rhs=Lp[sl][:ks, qoff:QT_W], start=False, stop=first,
                            skip_group_check=True),
                            reads=[r_Lp[sl], r_bank[zs]], writes=[r_bank[zs]])
                        if not first:
                            op = S.add("tensor", lambda e: e.matmul(
                                Zb[:ks, qoff:QT_W], lhsT=negOnes[:, :ks], rhs=Lrun[lc][:, qoff:QT_W],
                                start=False, stop=True, skip_group_check=True),
                                reads=[r_Lrun[lc]])
                            r_bank[zs].w = op
                        if not it["last"]:
                            S.add("vector", lambda e: e.tensor_tensor(
                                out=Lrun[1 - lc][:, qoff:QT_W], in0=Lrun[lc][:, qoff:QT_W],
                                in1=Lp[sl][:, qoff:QT_W], op=ALU.add),
                                reads=[r_Lrun[lc], r_Lp[sl]], writes=[r_Lrun[1 - lc]])
                            lcur[0] = 1 - lc

                    def st_e2(i):
                        it = items[i]
                        zs = 4 + (i % 3)
                        Zb = banks[zs]
                        sl = i % 2
                        ks, qoff = it["ks"], it["qoff"]
                        S.add("scalar", lambda e: e.activation(
                            out=Abuf[sl][:ks, 0, qoff:QT_W], in_=Zb[:ks, qoff:QT_W], func=AF.Exp),
                            reads=[r_bank[zs]], writes=[r_A[sl]])

                    def st_av(i):
                        it = items[i]
                        sl = i % 2
                        ks, qoff, kb, first, last, j = it["ks"], it["qoff"], it["kb"], it["first"], it["last"], it["j"]
                        ob = obank(j)
                        op = S.add("tensor", lambda e: e.matmul(
                            banks[ob][:, qoff:QT_W], lhsT=V[:ks, kb, :], rhs=Abuf[sl][:ks, 0, qoff:QT_W],
                            start=first, stop=last, skip_group_check=True),
                            reads=[r_A[sl], r_V], writes=[r_bank[ob]] if first else [])
                        if not first:
                            r_bank[ob].w = op
                        if last:
                            msl = mix_ctr[0] % 2
                            mix_ctr[0] += 1
                            q0, q1 = qcols(j, 0)
                            S.add("vector", lambda e: e.tensor_tensor(
                                out=mixo[msl][:, 0:QT_W], in0=banks[ob][:, 0:QT_W], in1=GT_[:, q0:q1], op=ALU.mult),
                                reads=[r_bank[ob], r_QKG], writes=[r_mixo[msl]], excl=[r_bank[ob]])
                            store_mix(hu, j, msl)

                    for t in range(-2, n + 1):
                        if 0 <= t < n:
                            st_ul(t)
                        if 0 <= t - 1 < n:
                            st_av(t - 1)
                        if 0 <= t + 2 < n:
                            st_qk(t + 2)
                        if 0 <= t + 1 < n:
                            st_e1ln(t + 1)
                        if 0 <= t < n:
                            st_e2(t)

                def attn_df(hu):
                    QT_, KT_, GT_ = QKG[:, 0, :], QKG[:, 1, :], QKG[:, 2, :]
                    items = make_items()
                    n = len(items)
                    oB = [4, 5]
                    sB = [6, 7]
                    W_ = QT_W
                    Pacc = Ebuf[0][:].bitcast(BF16).rearrange("p (c n) -> p c n", c=2)
                    A3 = [Abuf[0], Abuf[1], Ebuf[1][:].bitcast(BF16).rearrange("p (c n) -> p c n", c=2)]
                    r_A3 = [r_A[0], r_A[1], r_E[1]]
                    r_Pacc = [r_E[0], r_E[0]]

                    def st_qk(i):
                        it = items[i]
                        sl = i % 2
                        ks, qoff, k0, q0, q1, diag = it["ks"], it["qoff"], it["k0"], it["q0"], it["q1"], it["diag"]
                        for c in range(2):
                            zs = 2 * sl + c
                            Zb = banks[zs]
                            op = S.add("tensor", lambda e, Zb=Zb, c=c: e.matmul(
                                Zb[:ks, qoff:QT_W], lhsT=KT_[64 * c:64 * c + 64, k0:k0 + ks],
                                rhs=QT_[64 * c:64 * c + 64, q0:q1], start=True, stop=not diag),
                                reads=[r_QKG], writes=[r_bank[zs]])
                            if diag:
                                op = S.add("tensor", lambda e, Zb=Zb: e.matmul(
                                    Zb[:ks, qoff:qoff + 128], lhsT=ident[:, :ks], rhs=maskD[:, :], start=False, stop=True))
                                r_bank[zs].w = op

                    def st_exp(i):
                        asl = i % 3
                        it = items[i]
                        sl = i % 2
                        ks, qoff = it["ks"], it["qoff"]
                        for c in range(2):
                            zs = 2 * sl + c
                            Zb = banks[zs]
                            S.add("scalar", lambda e, Zb=Zb, c=c: e.activation(
                                out=A3[asl][:ks, c, qoff:QT_W], in_=Zb[:ks, qoff:QT_W], func=AF.Exp,
                                scale=DF_SCALE),
                                reads=[r_bank[zs]],
                                writes=([r_A3[asl]] + ([r_Lrun[asl]] if asl < 2 else [])) if c == 0 else [])
                            if c == 1:
                                r_A3[asl].w = S.streams["scalar"][-1]

                    def st_av(i):
                        it = items[i]
                        sl = i % 2
                        asl = i % 3
                        ks, qoff, kb, first, last, j = it["ks"], it["qoff"], it["kb"], it["first"], it["last"], it["j"]
                        for c in range(2):
                            op = S.add("tensor", lambda e, c=c: e.matmul(
                                banks[oB[c]][:, qoff:QT_W], lhsT=V[:ks, kb, :], rhs=A3[asl][:ks, c, qoff:QT_W],
                                start=first, stop=last, skip_group_check=True),
                                reads=[r_A3[asl], r_V], writes=[r_bank[oB[c]]] if first else [])
                            if not first:
                                r_bank[oB[c]].w = op
                        if last:
                            epilogue(j)

                    def st_acc(i):
                        it = items[i]
                        sl = i % 2
                        asl = i % 3
                        ks, qoff, first = it["ks"], it["qoff"], it["first"]
                        if first:
                            S.add("vector", lambda e: e.memset(Pacc[:, :, :], 0.0), writes=[r_Pacc[0]])
                        S.add("vector", lambda e: e.tensor_tensor(
                            out=Pacc[:ks, :, qoff:QT_W], in0=Pacc[:ks, :, qoff:QT_W], in1=A3[asl][:ks, :, qoff:QT_W],
                            op=ALU.add), reads=[r_A3[asl], r_Pacc[0]], writes=[r_Pacc[0]])

                    pending = []

                    def epilogue(j):
                        while pending:
                            pending.pop(0)()
                        q0, q1 = qcols(j, 0)
                        msl = mix_ctr[0] % 2
                        mix_ctr[0] += 1
                        for c in range(2):
                            S.add("tensor", lambda e, c=c: e.matmul(
                                banks[sB[c]][:, 0:QT_W], lhsT=ones[:, :], rhs=Pacc[:, c, 0:QT_W], start=True, stop=True),
                                reads=[r_Pacc[c]], writes=[r_bank[sB[c]]])
                        S.add("vector", lambda e: e.tensor_copy(out=ep[2][:, 0:W_], in_=banks[oB[0]][:, 0:W_]),
                              reads=[r_bank[oB[0]]], writes=[r_ep[2]], excl=[r_bank[oB[0]]])
                        S.add("vector", lambda e: e.tensor_copy(out=ep[1][:, 0:W_], in_=banks[oB[1]][:, 0:W_]),
                              reads=[r_bank[oB[1]]], writes=[r_ep[1]], excl=[r_bank[oB[1]]])

                        def s1():
                            S.add("scalar", lambda e: e.activation(out=ep[0][:, 0:W_], in_=banks[sB[0]][:, 0:W_], func=AF.Ln),
                                  reads=[r_bank[sB[0]]], writes=[r_ep[0]], excl=[r_bank[sB[0]]])
                            S.add("scalar", lambda e: e.activation(out=ep[0][:, 0:W_], in_=ep[0][:, 0:W_], func=AF.Exp, scale=-1.0),
                                  reads=[r_ep[0]], writes=[r_ep[0]])

                        def s2():
                            S.add("vector", lambda e: e.tensor_tensor(
                                out=ep[2][:, 0:W_], in0=ep[2][:, 0:W_], in1=ep[0][:, 0:W_], op=ALU.mult),
                                reads=[r_ep[2], r_ep[0]], writes=[r_ep[2]])

                        def s3():
                            S.add("scalar", lambda e: e.activation(out=ep[0][:, 0:W_], in_=banks[sB[1]][:, 0:W_], func=AF.Ln),
                                  reads=[r_bank[sB[1]]], writes=[r_ep[0]], excl=[r_bank[sB[1]]])
                            S.add("scalar", lambda e: e.activation(out=ep[0][:, 0:W_], in_=ep[0][:, 0:W_], func=AF.Exp, scale=-1.0),
                                  reads=[r_ep[0]], writes=[r_ep[0]])

                        def s4_():
                            S.add("vector", lambda e: e.tensor_tensor(
                                out=ep[1][:, 0:W_], in0=ep[1][:, 0:W_], in1=ep[0][:, 0:W_], op=ALU.mult),
                                reads=[r_ep[1], r_ep[0]], writes=[r_ep[1]])
                            S.add("vector", lambda e: e.scalar_tensor_tensor(
                                out=ep[2][:, 0:W_], in0=ep[1][:, 0:W_], scalar=neg_lam, in1=ep[2][:, 0:W_],
                                op0=ALU.mult, op1=ALU.add), reads=[r_ep[1], r_ep[2]], writes=[r_ep[2]])

                        def s5():
                            S.add("scalar", lambda e: e.activation(out=sqb[:, 0:W_], in_=ep[2][:, 0:W_], func=AF.Square),
                                  reads=[r_ep[2]], writes=[r_sqb])
                            S.add("tensor", lambda e: e.matmul(banks[sB[0]][:, 0:W_], lhsT=ones[:, :], rhs=sqb[:, 0:W_],
                                                               start=True, stop=True),
                                  reads=[r_sqb], writes=[r_bank[sB[0]]])

                        def s6():
                            S.add("vector", lambda e: e.tensor_scalar(
                                out=ep[0][:, 0:W_], in0=banks[sB[0]][:, 0:W_], scalar1=1.0 / 128.0, scalar2=SUBLN_EPS,
                                op0=ALU.mult, op1=ALU.add), reads=[r_bank[sB[0]]], writes=[r_ep[0]], excl=[r_bank[sB[0]]])
                            S.add("scalar", lambda e: e.activation(out=ep[1][:, 0:W_], in_=ep[0][:, 0:W_], func=AF.Ln),
                                  reads=[r_ep[0]], writes=[r_ep[1]])
                            S.add("scalar", lambda e: e.activation(out=ep[0][:, 0:W_], in_=ep[1][:, 0:W_], func=AF.Exp, scale=-0.5),
                                  reads=[r_ep[1]], writes=[r_ep[0]])

                        def s7():
                            S.add("vector", lambda e: e.tensor_tensor(
                                out=ep[1][:, 0:W_], in0=ep[2][:, 0:W_], in1=ep[0][:, 0:W_], op=ALU.mult),
                                reads=[r_ep[2], r_ep[0]], writes=[r_ep[1]])
                            S.add("vector", lambda e: e.scalar_tensor_tensor(
                                out=mixo[msl][:, 0:W_], in0=ep[1][:, 0:W_], scalar=sublnw_s, in1=GT_[:, q0:q1],
                                op0=ALU.mult, op1=ALU.mult), reads=[r_ep[1], r_QKG], writes=[r_mixo[msl]])
                            store_mix(hu, j, msl)

                        pending.extend([s1, s2, s3, s4_, s5, s6, s7])

                    for t in range(-1, n):
                        if 0 <= t + 1 < n:
                            st_qk(t + 1)
                        had_pending = len(pending) > 0
                        if 0 <= t < n:
                            st_av(t)
                        if 0 <= t + 1 < n:
                            st_exp(t + 1)
                            st_acc(t + 1)
                        if had_pending and (t % 2 == 0):
                            pending.pop(0)()
                    while pending:
                        pending.pop(0)()

                for ih, hu in enumerate(head_units):
                    if ih == 0:
                        load_W(hu)
                    inproj(hu)
                    if ih + 1 < len(head_units):
                        load_W(head_units[ih + 1])
                    if hu < 8:
                        attn_sb(hu)
                    else:
                        attn_df(hu)
                with nc.Block() as blkB:
                    S.emit(blkB, dma_counts)

        with ExitStack() as esC:
            def sbC(name, shape, dt):
                return esC.enter_context(nc.sbuf_tensor(name, shape, dt))
            Wo = sbC("Wo", [128, NCH, D_MODEL], BF16)
            mT = [sbC(f"mT_{i}", [128, NCH, 128], BF16) for i in range(2)]
            xr = [sbC(f"xr_{i}", [128, D_MODEL], F32) for i in range(2)]
            yo = [sbC(f"yo_{i}", [128, D_MODEL], F32) for i in range(2)]
            S = Sched(eng_sem_sets[2])
            r_Wo = Res()
            r_mT = [Res(), Res()]
            r_xr = [Res(), Res()]
            r_yo = [Res(), Res()]
            r_bank = [Res() for _ in range(8)]
            r_Woh = [Res(), Res()]
            wo_sems = [dsems[0], dsems[8]]
            for hh in range(2):
                for c in range(NCH):
                    srcw = wout_d[128 * c:128 * (c + 1), 1024 * hh:1024 * (hh + 1)]
                    S.add("gpsimd", lambda e, c=c, hh=hh, srcw=srcw: e.dma_start(
                        out=Wo[:, c, 1024 * hh:1024 * (hh + 1)], in_=srcw),
                        writes=[r_Woh[hh]] if c == 0 else [], dma_sem=wo_sems[hh])
                    if c != 0:
                        r_Woh[hh].w = S.streams["gpsimd"][-1]
            bc = 0
            for tb in range(SEQ // 128):
                sl = tb % 2
                S.add("sync", lambda e, sl=sl, tb=tb: e.dma_start(out=mT[sl][:], in_=mix_d[tb]),
                      writes=[r_mT[sl]], dma_sem=dsems[1 + sl])
                S.add("sync", lambda e, sl=sl, tb=tb: e.dma_start(out=xr[sl][:], in_=x_d[128 * tb:128 * (tb + 1), :]),
                      writes=[r_xr[sl]], dma_sem=dsems[3 + sl])
                for n in range(D_MODEL // 512):
                    bk = bc % 4
                    bc += 1
                    for c in range(NCH):
                        op = S.add("tensor", lambda e, bk=bk, sl=sl, c=c, n=n: e.matmul(
                            banks[bk][:, :], lhsT=mT[sl][:, c, :], rhs=Wo[:, c, 512 * n:512 * (n + 1)],
                            start=(c == 0), stop=(c == NCH - 1)),
                            reads=[r_mT[sl], r_Woh[n // 2]], writes=[r_bank[bk]] if c == 0 else [])
                        if c != 0:
                            r_bank[bk].w = op
                    op = S.add("vector", lambda e, bk=bk, sl=sl, n=n: e.tensor_tensor(
                        out=yo[sl][:, 512 * n:512 * (n + 1)], in0=banks[bk][:, :], in1=xr[sl][:, 512 * n:512 * (n + 1)],
                        op=ALU.add), reads=[r_bank[bk], r_xr[sl]], writes=[r_yo[sl]] if n == 0 else [])
                    if n != 0:
                        r_yo[sl].w = op
                S.add("scalar", lambda e, sl=sl, tb=tb: e.dma_start(out=out_d[128 * tb:128 * (tb + 1), :], in_=yo[sl][:]),
                      reads=[r_yo[sl]], dma_sem=dsems[5 + sl])
            with nc.Block() as blkC:
                S.emit(blkC, dma_counts)
    return nc


def host_constants(SEQ):
    NB = 1 + SEQ // 128
    cm = np.zeros((128, 6, 128), np.float32)
    i = np.arange(128)
    cm[:, 0, :] = np.eye(128, dtype=np.float32)
    cm[:, 1, :] = -(i[:, None] >= i[None, :]).astype(np.float32)
    cm[:, 2, :] = -1.0
    cm[:, 3, :] = 1.0
    cm[:, 4, :] = np.where(i[:, None] >= i[None, :], NEG, 0.0)
    cm[:, 5, :] = np.where(i[:, None] > i[None, :], NEG, 0.0)
    rope = np.zeros((128, 2, NB, 16), np.float32)
    inv_freq = (np.float32(ROPE_THETA) ** (-np.arange(0, 16, 2, dtype=np.float32) / np.float32(16))).astype(np.float32)
    for b in range(NB):
        t0, P = tok_block(b)
        pos = (t0 + np.arange(P)).astype(np.float32)
        ang = (pos[:, None] * inv_freq[None, :]).astype(np.float32)
        co, si = np.cos(ang), np.sin(ang)
        rope[:P, 0, b, 0:8] = co
        rope[:P, 0, b, 8:16] = co
        rope[:P, 1, b, 0:8] = -si
        rope[:P, 1, b, 8:16] = si
    return cm, rope


def kernel(x, meta, norm_w, w_in, q_norm_w, k_norm_w, lambda_q1, lambda_k1, lambda_q2, lambda_k2,
           subln_w, w_out, _head_units=None):
    x = np.asarray(x, np.float32)
    B, SEQ, _ = x.shape
    meta = np.ascontiguousarray(np.asarray(meta, np.float32))
    norm_w = np.asarray(norm_w, np.float32)[0]
    w_in = np.asarray(w_in, np.float32)[0]
    w_out = np.ascontiguousarray(np.asarray(w_out, np.float32)[0])
    cols = []
    for hu in range(16):
        base = 0 if hu < 8 else 4096
        h = hu % 8
        for part in range(4):
            cols.append(np.arange(base + part * 1024 + h * 128, base + part * 1024 + (h + 1) * 128))
    cols = np.concatenate(cols)
    w_in_r = np.ascontiguousarray(w_in[:, cols].reshape(D_MODEL, 16, 512).transpose(1, 0, 2))
    nw_bc = np.ascontiguousarray(np.broadcast_to(norm_w.reshape(NCH, 128).T[:, :, None], (128, NCH, 128))).astype(np.float32)
    params = np.zeros((128, 5, 128), np.float32)
    params[:, 0, :] = np.tile(np.asarray(q_norm_w, np.float32)[0], 2)[None, :]
    params[:, 1, :] = np.tile(np.asarray(k_norm_w, np.float32)[0], 2)[None, :]
    params[:, 2, :64] = np.asarray(lambda_q1, np.float32)[0][None, :]
    params[:, 2, 64:] = np.asarray(lambda_k1, np.float32)[0][None, :]
    params[:, 3, :64] = np.asarray(lambda_q2, np.float32)[0][None, :]
    params[:, 3, 64:] = np.asarray(lambda_k2, np.float32)[0][None, :]
    params[:, 4, 0] = np.asarray(subln_w, np.float32)[0]
    cm, rope = host_constants(SEQ)

    nc = build_program(SEQ, _head_units)
    in_maps = []
    for b in range(B):
        in_maps.append({
            "x": np.ascontiguousarray(x[b]), "meta": meta, "nw_bc": nw_bc, "w_in_r": w_in_r, "w_out": w_out,
            "cmats": cm, "rope": rope, "params": params,
        })
    res = run_bass_kernel_spmd(nc, in_maps, core_ids=list(range(B)))
    out = np.stack([np.asarray(r["out"], np.float32).reshape(SEQ, D_MODEL) for r in res.results], axis=0)
    return out
```
